# Optimizing a Trainium2 kernel written in Bass

```python
import jax, jax.numpy as jnp
from jax import lax
import numpy as np

D_MODEL = 2048
BATCH = 4
SEQ = 4096
DEPTH = 2

D_LRU = 1024
LRU_BLOCKS = 16
LRU_BW = D_LRU // LRU_BLOCKS
CONV_WIDTH = 4
LRU_C = 8.0
N_HEADS = 8
HEAD_DIM = 128
D_ATTN = N_HEADS * HEAD_DIM
MOBA_BLOCK = 256
MOBA_TOPK = 3
Q_CHUNK = 32
ROPE_THETA = 500000.0
ROT_DIM = HEAD_DIM // 4
D_FF = 5632
NORM_EPS = 1e-6
IN_COLS = D_LRU + 3 * D_ATTN + 2 * D_MODEL

kernel_name = "hybrid_rglru_moba_macaron"


def rms_norm(x, g):
    xf = x.astype(jnp.float32)
    y = xf * lax.rsqrt(jnp.mean(xf * xf, axis=-1, keepdims=True) + NORM_EPS)
    return (y * g.astype(jnp.float32)).astype(x.dtype)


def swiglu(x, w13, w2):
    a, b = jnp.split(x @ w13, 2, axis=-1)
    return (jax.nn.silu(a) * b) @ w2


def causal_conv(u, w, b):
    s = u.shape[1]
    up = jnp.pad(u, ((0, 0), (CONV_WIDTH - 1, 0), (0, 0)))
    y = b
    for tap in range(CONV_WIDTH):
        y = y + up[:, tap:tap + s] * w[tap]
    return y


def rg_lru(u, wa, ba, wx, bx, lam):
    bsz, s, _ = u.shape
    uf = u.astype(jnp.float32)
    ub = uf.reshape(bsz, s, LRU_BLOCKS, LRU_BW)
    r = jax.nn.sigmoid(jnp.einsum('bsni,nij->bsnj', ub, wa.astype(jnp.float32)).reshape(bsz, s, D_LRU) + ba.astype(jnp.float32))
    i = jax.nn.sigmoid(jnp.einsum('bsni,nij->bsnj', ub, wx.astype(jnp.float32)).reshape(bsz, s, D_LRU) + bx.astype(jnp.float32))
    log_a = -LRU_C * r * jax.nn.softplus(-lam.astype(jnp.float32))
    a = jnp.exp(log_a)
    b = jnp.sqrt(-jnp.expm1(2.0 * log_a)) * (i * uf)

    def combine(left, right):
        a_l, b_l = left
        a_r, b_r = right
        return a_l * a_r, a_r * b_l + b_r

    _, h = lax.associative_scan(combine, (a, b), axis=1)
    return h.astype(u.dtype)


def rope_tables(s):
    pos = jnp.arange(s, dtype=jnp.float32)
    inv = ROPE_THETA ** (-jnp.arange(0, ROT_DIM, 2, dtype=jnp.float32) / ROT_DIM)
    ang = pos[:, None] * inv[None, :]
    return jnp.cos(ang), jnp.sin(ang)


def partial_rotary(x, cos, sin):
    cos = cos.astype(x.dtype)
    sin = sin.astype(x.dtype)
    x1 = x[..., :ROT_DIM // 2]
    x2 = x[..., ROT_DIM // 2:ROT_DIM]
    return jnp.concatenate([x1 * cos - x2 * sin, x2 * cos + x1 * sin, x[..., ROT_DIM:]], axis=-1)


def moba_attention(q, k, v):
    bsz, nh, s, dh = q.shape
    nb = -(-s // MOBA_BLOCK)
    sp = nb * MOBA_BLOCK
    padw = ((0, 0), (0, 0), (0, sp - s), (0, 0))
    q = jnp.pad(q, padw)
    k = jnp.pad(k, padw)
    v = jnp.pad(v, padw)
    kb = k.reshape(bsz, nh, nb, MOBA_BLOCK, dh)
    vb = v.reshape(bsz, nh, nb, MOBA_BLOCK, dh)
    n_sel = min(MOBA_TOPK, nb - 1)
    scale = dh ** -0.5
    nc = sp // Q_CHUNK
    qc = q.reshape(bsz, nh, nc, Q_CHUNK, dh).transpose(2, 0, 1, 3, 4)
    chunk_ids = jnp.arange(nc)

    if n_sel > 0:
        kmean = jnp.mean(kb.astype(jnp.float32), axis=3)
        gate = jnp.einsum('bhsd,bhnd->bhsn', q.astype(jnp.float32), kmean)
        q_blk = jnp.arange(sp) // MOBA_BLOCK
        past = jnp.arange(nb)[None, :] < q_blk[:, None]
        gate = jnp.where(past, gate, -jnp.inf)
        _, sel = lax.top_k(gate, n_sel)
        selc = sel.reshape(bsz, nh, nc, Q_CHUNK, n_sel).transpose(2, 0, 1, 3, 4)
    else:
        selc = jnp.zeros((nc, bsz, nh, Q_CHUNK, 0), jnp.int32)
    bi = jnp.arange(bsz)[:, None, None, None]
    hi = jnp.arange(nh)[None, :, None, None]

    def chunk_attn(args):
        c, qi, si = args
        qpos = c * Q_CHUNK + jnp.arange(Q_CHUNK)
        j = (c * Q_CHUNK) // MOBA_BLOCK
        k_own = lax.dynamic_index_in_dim(kb, j, axis=2, keepdims=False)
        v_own = lax.dynamic_index_in_dim(vb, j, axis=2, keepdims=False)
        kpos = j * MOBA_BLOCK + jnp.arange(MOBA_BLOCK)
        s_own = jnp.einsum('bhqd,bhtd->bhqt', qi, k_own).astype(jnp.float32) * scale
        s_own = jnp.where(kpos[None, :] <= qpos[:, None], s_own, -jnp.inf)
        if n_sel > 0:
            k_sel = kb[bi, hi, si]
            v_sel = vb[bi, hi, si]
            s_sel = jnp.einsum('bhqd,bhqntd->bhqnt', qi, k_sel).astype(jnp.float32) * scale
            valid = jnp.arange(n_sel) < j
            s_sel = jnp.where(valid[:, None], s_sel, -jnp.inf)
            s_all = jnp.concatenate([s_sel.reshape(bsz, nh, Q_CHUNK, n_sel * MOBA_BLOCK), s_own], axis=-1)
            p = jax.nn.softmax(s_all, axis=-1).astype(v.dtype)
            p_sel = p[..., :n_sel * MOBA_BLOCK].reshape(bsz, nh, Q_CHUNK, n_sel, MOBA_BLOCK)
            p_own = p[..., n_sel * MOBA_BLOCK:]
            return (jnp.einsum('bhqnt,bhqntd->bhqd', p_sel, v_sel)
                    + jnp.einsum('bhqt,bhtd->bhqd', p_own, v_own))
        p_own = jax.nn.softmax(s_own, axis=-1).astype(v.dtype)
        return jnp.einsum('bhqt,bhtd->bhqd', p_own, v_own)

    out = lax.map(chunk_attn, (chunk_ids, qc, selc))
    out = out.transpose(1, 2, 0, 3, 4).reshape(bsz, nh, sp, dh)
    return out[:, :, :s]


def hybrid_mixer(h, w_in, b_gate, conv_w, conv_b, wa, ba, wx, bx, lam,
                 w_lru_up, w_attn_up, w_out, cos, sin):
    bsz, s, _ = h.shape
    z = h @ w_in
    u, q, k, v, g = jnp.split(z, [D_LRU, D_LRU + D_ATTN, D_LRU + 2 * D_ATTN, D_LRU + 3 * D_ATTN], axis=-1)
    y_a = rg_lru(causal_conv(u, conv_w, conv_b), wa, ba, wx, bx, lam) @ w_lru_up
    heads = lambda t: t.reshape(bsz, s, N_HEADS, HEAD_DIM).transpose(0, 2, 1, 3)
    qh = partial_rotary(heads(q), cos, sin)
    kh = partial_rotary(heads(k), cos, sin)
    o = moba_attention(qh, kh, heads(v)).transpose(0, 2, 1, 3).reshape(bsz, s, D_ATTN)
    y_b = o @ w_attn_up
    g_a, g_b = jnp.split(jax.nn.sigmoid(g + b_gate), 2, axis=-1)
    return (g_a * y_a + g_b * y_b) @ w_out


def setup_inputs(seed: int = 0) -> dict:
    key = jax.random.key(seed)
    ks = jax.random.split(key, 18)
    f32 = jnp.float32
    nrm = lambda k, shape, scale: jax.random.normal(k, shape, f32) * scale
    x = nrm(ks[0], (BATCH, SEQ, D_MODEL), 1.0)
    norm_gains = 1.0 + nrm(ks[1], (DEPTH, 6, D_MODEL), 0.02)
    ffn_w13 = nrm(ks[2], (DEPTH, 2, D_MODEL, 2 * D_FF), D_MODEL ** -0.5)
    ffn_w2 = nrm(ks[3], (DEPTH, 2, D_FF, D_MODEL), D_FF ** -0.5)
    w_in = nrm(ks[4], (DEPTH, D_MODEL, IN_COLS), D_MODEL ** -0.5)
    b_gate = nrm(ks[5], (DEPTH, 2 * D_MODEL), 0.02)
    conv_w = nrm(ks[6], (DEPTH, CONV_WIDTH, D_LRU), CONV_WIDTH ** -0.5)
    conv_b = nrm(ks[7], (DEPTH, D_LRU), 0.02)
    lru_wa = nrm(ks[8], (DEPTH, LRU_BLOCKS, LRU_BW, LRU_BW), LRU_BW ** -0.5)
    lru_ba = nrm(ks[9], (DEPTH, D_LRU), 0.02)
    lru_wx = nrm(ks[10], (DEPTH, LRU_BLOCKS, LRU_BW, LRU_BW), LRU_BW ** -0.5)
    lru_bx = nrm(ks[11], (DEPTH, D_LRU), 0.02)
    a_c = jax.random.uniform(ks[12], (DEPTH, D_LRU), f32, 0.9, 0.999)
    sig = a_c ** (1.0 / LRU_C)
    lru_lambda = jnp.log(sig) - jnp.log1p(-sig)
    w_lru_up = nrm(ks[13], (DEPTH, D_LRU, D_MODEL), D_LRU ** -0.5)
    w_attn_up = nrm(ks[14], (DEPTH, D_ATTN, D_MODEL), D_ATTN ** -0.5)
    w_out = nrm(ks[15], (DEPTH, D_MODEL, D_MODEL), D_MODEL ** -0.5)
    return {"x": x, "norm_gains": norm_gains, "ffn_w13": ffn_w13, "ffn_w2": ffn_w2,
            "w_in": w_in, "b_gate": b_gate, "conv_w": conv_w, "conv_b": conv_b,
            "lru_wa": lru_wa, "lru_ba": lru_ba, "lru_wx": lru_wx, "lru_bx": lru_bx,
            "lru_lambda": lru_lambda, "w_lru_up": w_lru_up, "w_attn_up": w_attn_up,
            "w_out": w_out}


def reference(x, norm_gains, ffn_w13, ffn_w2, w_in, b_gate, conv_w, conv_b,
              lru_wa, lru_ba, lru_wx, lru_bx, lru_lambda, w_lru_up, w_attn_up, w_out):
    cos, sin = rope_tables(x.shape[1])
    for l in range(DEPTH):
        ng = norm_gains[l]
        x = x + 0.5 * rms_norm(swiglu(rms_norm(x, ng[0]), ffn_w13[l, 0], ffn_w2[l, 0]), ng[1])
        m = hybrid_mixer(rms_norm(x, ng[2]), w_in[l], b_gate[l], conv_w[l], conv_b[l],
                         lru_wa[l], lru_ba[l], lru_wx[l], lru_bx[l], lru_lambda[l],
                         w_lru_up[l], w_attn_up[l], w_out[l], cos, sin)
        x = x + rms_norm(m, ng[3])
        x = x + 0.5 * rms_norm(swiglu(rms_norm(x, ng[4]), ffn_w13[l, 1], ffn_w2[l, 1]), ng[5])
    return x
```

```python
import contextlib
import numpy as np
import concourse.bass as bass
import concourse.mybir as mybir
from concourse.bass_utils import run_bass_kernel_spmd

F32 = mybir.dt.float32
BF16 = mybir.dt.bfloat16
AF = mybir.ActivationFunctionType
ALU = mybir.AluOpType
AX = mybir.AxisListType

D = 2048
KC = 16
DFF = 5632
FC = 44
DEPTH = 2
SEQ = 4096
BATCH = 4
D_LRU = 1024
D_ATTN = 1024
NH = 8
HD = 128
MB = 256
NBLK = SEQ // MB
IN_COLS = 8192
EPS = 1e-6
TB = 1024
NTT = TB // 512
NEG = -30000.0


class _Eng:
    def __init__(self, name):
        self.name = name
        self.items = []
        self.sem = None
        self.count = 0
        self.waited = {}
        self.last = None


class Builder:
    ENG = ("pe", "act", "dve", "pool", "sp")

    def __init__(self, nc):
        self.nc = nc
        self.engs = {n: _Eng(n) for n in self.ENG}
        for n in self.ENG:
            self.engs[n].sem = nc.alloc_semaphore("prog_" + n)
        self.nsem = 0
        self.dma_pending = {}
        self.swdge_fifo = []
        self.swdge_out = 0

    def _waits(self, e, waits):
        ws = []
        for t in waits:
            if t is None:
                continue
            sem, val, owner = t
            if owner == e.name:
                continue
            key = id(sem)
            if e.waited.get(key, 0) >= val:
                continue
            e.waited[key] = val
            ws.append((sem, val))
        return ws

    def op(self, eng, fn, waits=(), signal=True):
        e = self.engs[eng]
        ws = self._waits(e, waits)
        tok = None
        inc = None
        if signal:
            e.count += 1
            tok = (e.sem, e.count, eng)
            inc = (e.sem, 1)
            e.last = tok
        e.items.append((fn, ws, inc))
        return tok

    def new_dsem(self, name=None):
        self.nsem += 1
        h = self.nc.alloc_semaphore(name or ("dsem%d" % self.nsem))
        return [h, 0]

    def dma(self, queue, out, in_, dsem, waits=(), **kw):
        e = self.engs[queue]
        waits = list(waits)
        if queue == "pool":
            nd = 1
            for d_ in out.shape[:-1]:
                nd *= d_
            nd = (nd + 15) // 16 + 2
            while self.swdge_fifo and self.swdge_out + nd > 640:
                t_old, n_old = self.swdge_fifo.pop(0)
                self.swdge_out -= n_old
                waits.append(t_old)
        ws = self._waits(e, waits)
        dsem[1] += 16
        tok = (dsem[0], dsem[1], "dma")
        if queue == "pool":
            self.swdge_fifo.append((tok, nd))
            self.swdge_out += nd
        self.dma_pending[id(dsem[0])] = tok
        e.items.append((lambda eng, o=out, i=in_, k=kw: eng.dma_start(out=o, in_=i, **k), ws, (dsem[0], 16)))
        return tok

    def wait_only(self, eng, waits):
        e = self.engs[eng]
        ws = self._waits(e, waits)
        if ws:
            e.items.append((None, ws, None))

    def barrier(self):
        toks = [self.engs[n].last for n in self.ENG] + list(self.dma_pending.values())
        self.dma_pending = {}
        for n in self.ENG:
            self.wait_only(n, toks)

    def emit(self):
        nc = self.nc
        engs = self.engs

        def replay(e, engobj):
            for fn, ws, inc in e.items:
                for sem, val in ws:
                    engobj.wait_ge(sem, val)
                if fn is None:
                    continue
                ins = fn(engobj)
                if inc is not None:
                    ins.then_inc(inc[0], inc[1])

        with nc.Block() as block:
            @block.tensor
            def _(pe):
                replay(engs["pe"], pe)

            @block.scalar
            def _(act):
                replay(engs["act"], act)

            @block.vector
            def _(dve):
                replay(engs["dve"], dve)

            @block.gpsimd
            def _(pool):
                replay(engs["pool"], pool)

            @block.sync
            def _(sp):
                replay(engs["sp"], sp)


class Arena:
    def __init__(self, handle, nbytes):
        self.h = handle
        self.nbytes = nbytes

    def at(self, off, dtype, shape):
        esz = 4 if dtype == F32 else 2
        n = 1
        for s in shape:
            n *= s
        nb = n * esz
        assert off % 4 == 0 and off + nb <= self.nbytes, (off, nb, self.nbytes)
        v = self.h[:, off // 2: (off + nb) // 2]
        if dtype != BF16:
            v = v.bitcast(dtype)
        if len(shape) == 2:
            return v.rearrange("p (a b) -> p a b", a=shape[0])
        if len(shape) == 3:
            return v.rearrange("p (a b c) -> p a b c", a=shape[0], b=shape[1])
        return v


class Ctx:
    pass


def stream(n, nslots, load_fn, compute_fn, first_waits=()):
    loaded = {}
    for u in range(min(nslots, n)):
        loaded[u] = load_fn(u, u % nslots, list(first_waits))
    for u in range(n):
        done = compute_fn(u, u % nslots, loaded.pop(u))
        nxt = u + nslots
        if nxt < n:
            loaded[nxt] = load_fn(nxt, nxt % nslots, [done])


def bank_wait(C, i):
    return C.bank_free[i]


def emit_stats_finish(C, psS, rstd, waits):
    b = C.b
    toks = []
    for tt in range(NTT):
        sl = slice(tt * 512, (tt + 1) * 512)
        t1 = b.op("act", lambda e, tt=tt, sl=sl: e.activation(out=rstd[:, sl], in_=C.ps[psS[tt]][:, :], func=AF.Sqrt,
                                                              scale=1.0 / D, bias=C.eps[:, 0:1]), waits=waits)
        C.bank_free[psS[tt]] = t1
        t2 = b.op("dve", lambda e, sl=sl: e.reciprocal(out=rstd[:, sl], in_=rstd[:, sl]), waits=[t1])
        toks.append(t2)
    return toks


def emit_prenorm(C, xin_v, t0, gidx, xn, xs, rstd, sq, psS):
    b = C.b
    nxs = len(xs)
    last_mm = None
    for kc in range(KC):
        s = kc % nxs
        t_ld = b.dma("sp", xs[s], xin_v[:, kc, t0:t0 + TB], C.xs_sem[s], waits=[C.xs_free[s]])
        q = kc % 2
        t_sq = b.op("act", lambda e, s=s, q=q: e.activation(out=sq[q], in_=xs[s], func=AF.Square),
                    waits=[t_ld, C.sq_free[q]])
        C.xs_free[s] = t_sq
        for tt in range(NTT):
            w = [t_sq]
            if kc == 0:
                w.append(C.bank_free[psS[tt]])
            last_mm = b.op("pe", lambda e, tt=tt, q=q, kc=kc: e.matmul(
                C.ps[psS[tt]][:, :], lhsT=C.ones_f[:, :], rhs=sq[q][:, tt * 512:(tt + 1) * 512],
                start=(kc == 0), stop=(kc == KC - 1)), waits=w, signal=True)
        C.sq_free[q] = last_mm
    t_r = emit_stats_finish(C, psS, rstd, [last_mm])
    toks = []
    for kc in range(KC):
        s = kc % nxs
        t_ld = b.dma("sp", xs[s], xin_v[:, kc, t0:t0 + TB], C.xs_sem[s], waits=[C.xs_free[s]])
        t_n = b.op("dve", lambda e, s=s, kc=kc: e.scalar_tensor_tensor(
            out=xn[:, kc, :], in0=xs[s], scalar=C.ng[:, gidx, kc:kc + 1], in1=rstd, op0=ALU.mult, op1=ALU.mult),
            waits=[t_ld] + t_r)
        C.xs_free[s] = t_n
        toks.append(t_n)
    return toks


def emit_post_residual(C, y, gidx, rstd, t_rstd, xin_v, xout_v, t0, xs, rscale):
    b = C.b
    nxs = len(xs)
    for m in range(KC):
        s = m % nxs
        t_ld = b.dma("sp", xs[s], xin_v[:, m, t0:t0 + TB], C.xs_sem[s], waits=[C.xs_free[s]])
        t_a = b.op("dve", lambda e, m=m: e.scalar_tensor_tensor(
            out=y[:, m, :], in0=y[:, m, :], scalar=C.ng[:, gidx, m:m + 1], in1=rstd, op0=ALU.mult, op1=ALU.mult),
            waits=t_rstd)
        t_b = b.op("dve", lambda e, m=m, s=s: e.scalar_tensor_tensor(
            out=xs[s], in0=y[:, m, :], scalar=float(rscale), in1=xs[s], op0=ALU.mult, op1=ALU.add), waits=[t_ld])
        t_st = b.dma("sp", xout_v[:, m, t0:t0 + TB], xs[s], C.xs_sem[s], waits=[t_b])
        C.xs_free[s] = t_st


def emit_outproj_stats(C, m, psY, psS, y, sq, t_mm):
    b = C.b
    last = None
    for tt in range(NTT):
        sl = slice(tt * 512, (tt + 1) * 512)
        t_c = b.op("act", lambda e, tt=tt, sl=sl, m=m: e.activation(out=y[:, m, sl], in_=C.ps[psY[tt]][:, :], func=AF.Copy),
                   waits=[t_mm])
        C.bank_free[psY[tt]] = t_c
        q = tt
        t_q = b.op("dve", lambda e, sl=sl, q=q, m=m: e.tensor_tensor(out=sq[q][:, 0:512], in0=y[:, m, sl],
                                                                   in1=y[:, m, sl], op=ALU.mult),
                   waits=[t_c, C.sq_free[q]])
        w = [t_q]
        if m == 0:
            w.append(C.bank_free[psS[tt]])
        last = b.op("pe", lambda e, tt=tt, q=q, m=m: e.matmul(
            C.ps[psS[tt]][:, :], lhsT=C.ones_f[:, :], rhs=sq[q][:, 0:512], start=(m == 0), stop=(m == KC - 1)), waits=w)
        C.sq_free[q] = last
    return last


def ffn_block(C, xin, xout, t0, gpre, gpost, w13, w2):
    b = C.b
    A = C.arena
    xin_v = xin.rearrange("(kc p) t -> p kc t", p=128)
    xout_v = xout.rearrange("(kc p) t -> p kc t", p=128)
    w13v = w13.rearrange("(kc p) n -> p kc n", p=128)
    w2v = w2.rearrange("(fc p) n -> p fc n", p=128)
    xn = A.at(0, BF16, [KC, TB])
    y = A.at(0, F32, [KC, TB])
    w13s = [(A.at(32768 + s * 16384, BF16, [KC, 256]), A.at(32768 + s * 16384 + 8192, BF16, [KC, 256])) for s in range(2)]
    g = A.at(65536, BF16, [FC, TB])
    o = 65536 + FC * TB * 2
    w2s = [A.at(o + s * 11264, BF16, [FC, 128]) for s in range(2)]
    o += 2 * 11264
    xs = [A.at(o + s * 4096, F32, [TB]) for s in range(3)]
    o += 3 * 4096
    rstd = A.at(o, F32, [TB])
    o += 4096
    sq = [A.at(o + s * 4096, F32, [TB]) for s in range(2)]
    tmp = [A.at(o + s * 2048, F32, [512]) for s in range(4)]
    o += 8192
    assert o <= A.nbytes, o

    t_xn = emit_prenorm(C, xin_v, t0, gpre, xn, xs, rstd, sq, psS=[6, 7])

    w2_loaded = {}

    def w2_load(m, s, waits):
        return [b.dma("pool", w2s[s], w2v[:, :, m * 128:(m + 1) * 128], C.w2_sem[s], waits=waits)]

    tmp_free = [None] * 4
    state = {"tmpi": 0}

    def a_load(u, s, waits):
        b.dma("pool", w13s[s][0], w13v[:, :, u * 256:(u + 1) * 256], C.w13_sem[s], waits=waits)
        t = b.dma("pool", w13s[s][1], w13v[:, :, DFF + u * 256:DFF + (u + 1) * 256], C.w13_sem[s], waits=waits)
        return [t]

    def a_compute(u, s, ld):
        last = None
        for j in range(2):
            c = u * 2 + j
            base = (c % 2) * 4
            toks_ab = []
            for wi in range(2):
                wt = w13s[s][wi]
                for kc in range(KC):
                    for tt in range(NTT):
                        bank = base + wi * 2 + tt
                        w = []
                        if kc == 0:
                            w = list(ld) + [C.bank_free[bank]] + ([t_xn[-1]] if (u == 0 and j == 0 and wi == 0) else [])
                        last = b.op("pe", lambda e, bank=bank, wt=wt, kc=kc, tt=tt, j=j: e.matmul(
                            C.ps[bank][:, :], lhsT=wt[:, kc, j * 128:(j + 1) * 128], rhs=xn[:, kc, tt * 512:(tt + 1) * 512],
                            start=(kc == 0), stop=(kc == KC - 1)), waits=w, signal=(kc == KC - 1))
                toks_ab.append(last)
            for tt in range(NTT):
                ti = state["tmpi"] % 4
                state["tmpi"] += 1
                bankA = base + tt
                bankB = base + 2 + tt
                t_s = b.op("act", lambda e, ti=ti, bankA=bankA: e.activation(out=tmp[ti], in_=C.ps[bankA][:, :], func=AF.Silu),
                           waits=[toks_ab[0], tmp_free[ti]])
                C.bank_free[bankA] = t_s
                t_g = b.op("dve", lambda e, ti=ti, bankB=bankB, c=c, tt=tt: e.tensor_tensor(
                    out=g[:, c, tt * 512:(tt + 1) * 512], in0=tmp[ti], in1=C.ps[bankB][:, :], op=ALU.mult),
                    waits=[t_s, toks_ab[1]])
                tmp_free[ti] = t_g
                C.bank_free[bankB] = t_g
        if u == 0:
            for m in range(2):
                w2_loaded[m] = w2_load(m, m % 2, [])
        return last

    stream(DFF // 256, 2, a_load, a_compute)

    psS = [6, 7]
    stat = {"last": None}

    def b_compute(m, s, ld):
        par = m % 2
        psY = [par * 2 + tt for tt in range(NTT)]
        last = None
        for fc in range(FC):
            for tt in range(NTT):
                w = []
                if fc == 0:
                    w = list(ld) + [C.bank_free[psY[tt]]]
                last = b.op("pe", lambda e, tt=tt, fc=fc, s=s, bank=psY[tt]: e.matmul(
                    C.ps[bank][:, :], lhsT=w2s[s][:, fc, :], rhs=g[:, fc, tt * 512:(tt + 1) * 512],
                    start=(fc == 0), stop=(fc == FC - 1)), waits=w, signal=(fc == FC - 1))
        stat["last"] = emit_outproj_stats(C, m, psY, psS, y, sq, last)
        return last

    for m in range(KC):
        s = m % 2
        ld = w2_loaded.pop(m)
        done = b_compute(m, s, ld)
        if m + 2 < KC:
            w2_loaded[m + 2] = w2_load(m + 2, s, [done])

    t_r = emit_stats_finish(C, psS, rstd, [stat["last"]])
    emit_post_residual(C, y, gpost, rstd, t_r, xin_v, xout_v, t0, xs, 0.5)
    b.barrier()


def mix_inproj_block(C, xin, t0, l, w_in, S):
    b = C.b
    A = C.arena
    T = S["T"]
    xin_v = xin.rearrange("(kc p) t -> p kc t", p=128)
    wv = w_in.rearrange("(kc p) n -> p kc n", p=128)
    o = 0
    hn = A.at(o, BF16, [KC, TB]); o += KC * TB * 2
    ws = [A.at(o + s * 8192, BF16, [KC, 256]) for s in range(3)]; o += 3 * 8192
    xs = [A.at(o + s * 4096, F32, [TB]) for s in range(3)]; o += 3 * 4096
    rstd = A.at(o, F32, [TB]); o += 4096
    sq = [A.at(o + s * 4096, F32, [TB]) for s in range(2)]; o += 8192
    ust = [A.at(o + s * 4096, F32, [TB]) for s in range(2)]; o += 8192
    zf = [A.at(o + s * 2048, F32, [512]) for s in range(2)]; o += 4096
    t1 = A.at(o, F32, [512]); o += 2048
    t2 = A.at(o, F32, [512]); o += 2048
    qko = [A.at(o + s * 2048, BF16, [TB]) for s in range(2)]; o += 4096
    vst = [A.at(o + s * 4096, BF16, [8, 256]) for s in range(2)]; o += 8192
    rC = A.at(o, F32, [TB]); o += 4096
    rS = A.at(o, F32, [TB]); o += 4096
    assert o <= A.nbytes

    t_rc = b.dma("sp", rC[0:32, :], S["ropeC"][:, t0:t0 + TB], C.misc_sem[0])
    t_rs = b.dma("sp", rS[0:32, :], S["ropeS"][:, t0:t0 + TB], C.misc_sem[1])
    t_xn = emit_prenorm(C, xin_v, t0, l * 6 + 2, hn, xs, rstd, sq, psS=[6, 7])

    units = [("u", 0 + i * 256, i) for i in range(4)] + [("k", 2048 + i * 256, i) for i in range(4)] + \
            [("v", 3072 + i * 256, i) for i in range(4)] + [("q", 1024 + i * 256, i) for i in range(4)]
    st = {"ust": 0, "zf": 0, "qko": 0, "vst": 0, "vbank": 0, "par": 0}
    ust_free = [None, None]
    zf_free = [None, None]
    qko_free = [None, None]
    vst_free = [None, None]
    t12_free = [None]

    def load(u, s, waits):
        kind, c0, _ = units[u]
        return [b.dma("pool", ws[s], wv[:, :, c0:c0 + 256], C.w13_sem[s], waits=waits)]

    def compute(u, s, ld):
        kind, c0, ui = units[u]
        wt = ws[s]
        last = None
        first = [True]

        def fw(bank):
            w = [C.bank_free[bank]]
            if first[0]:
                w += list(ld)
                if u == 0:
                    w.append(t_xn[-1])
                first[0] = False
            return w

        if kind == "v":
            vs = st["vst"] % 2
            st["vst"] += 1
            for i in range(TB // 128):
                bank = 4 + (st["vbank"] % 2)
                st["vbank"] += 1
                for kc in range(KC):
                    w = fw(bank) if kc == 0 else []
                    if kc == 0 and i == 0:
                        w.append(vst_free[vs])
                    last = b.op("pe", lambda e, bank=bank, kc=kc, i=i: e.matmul(
                        C.ps[bank][:, 0:256], lhsT=hn[:, kc, i * 128:(i + 1) * 128], rhs=wt[:, kc, :],
                        start=(kc == 0), stop=(kc == KC - 1)), waits=w, signal=(kc == KC - 1))
                t_c = b.op("act", lambda e, bank=bank, i=i, vs=vs: e.activation(out=vst[vs][:, i, :], in_=C.ps[bank][:, 0:256],
                                                                              func=AF.Copy), waits=[last, vst_free[vs]])
                C.bank_free[bank] = t_c
            vdst = S["v_s"].rearrange("(i p) c -> p i c", p=128)[:, t0 // 128:(t0 + TB) // 128, ui * 256:(ui + 1) * 256]
            vst_free[vs] = b.dma("sp", vdst, vst[vs], C.misc_sem[2 + vs], waits=[t_c])
            return last

        for j in range(2):
            c = ui * 2 + j
            par = st["par"] % 2
            st["par"] += 1
            banks = [par * 2 + tt for tt in range(NTT)]
            for kc in range(KC):
                for tt in range(NTT):
                    w = fw(banks[tt]) if kc == 0 else []
                    last = b.op("pe", lambda e, bank=banks[tt], kc=kc, tt=tt, j=j: e.matmul(
                        C.ps[bank][:, :], lhsT=wt[:, kc, j * 128:(j + 1) * 128], rhs=hn[:, kc, tt * 512:(tt + 1) * 512],
                        start=(kc == 0), stop=(kc == KC - 1)), waits=w, signal=(kc == KC - 1))
            if kind == "u":
                us = st["ust"] % 2
                st["ust"] += 1
                for tt in range(NTT):
                    t_c = b.op("act", lambda e, bank=banks[tt], tt=tt, us=us: e.activation(
                        out=ust[us][:, tt * 512:(tt + 1) * 512], in_=C.ps[bank][:, :], func=AF.Copy),
                        waits=[last, ust_free[us]])
                    C.bank_free[banks[tt]] = t_c
                ust_free[us] = b.dma("sp", S["u_s"][c * 128:(c + 1) * 128, t0:t0 + TB], ust[us], C.misc_sem[4 + us], waits=[t_c])
            else:
                qs = st["qko"] % 2
                st["qko"] += 1
                t_last = None
                for tt in range(NTT):
                    zs = st["zf"] % 2
                    st["zf"] += 1
                    sl = slice(tt * 512, (tt + 1) * 512)
                    t_c = b.op("act", lambda e, bank=banks[tt], zs=zs: e.activation(out=zf[zs], in_=C.ps[bank][:, :], func=AF.Copy),
                               waits=[last, zf_free[zs]])
                    C.bank_free[banks[tt]] = t_c
                    rb = 4 + (st["vbank"] % 2)
                    st["vbank"] += 1
                    t_p = b.op("pe", lambda e, rb=rb, zs=zs: e.matmul(C.ps[rb][0:32, :], lhsT=C.perm[0:32, 0:32], rhs=zf[zs][0:32, :],
                                                                        start=True, stop=True), waits=[t_c, C.bank_free[rb]])
                    b.op("dve", lambda e, zs=zs, sl=sl: e.tensor_tensor(out=t1[0:32, :], in0=zf[zs][0:32, :], in1=rC[0:32, sl], op=ALU.mult),
                         waits=[t_c, t_rc, t12_free[0]])
                    t_2 = b.op("dve", lambda e, rb=rb, sl=sl: e.tensor_tensor(out=t2[0:32, :], in0=C.ps[rb][0:32, :], in1=rS[0:32, sl], op=ALU.mult),
                               waits=[t_p, t_rs])
                    C.bank_free[rb] = t_2
                    t_a = b.op("dve", lambda e, zs=zs: e.tensor_tensor(out=zf[zs][0:32, :], in0=t1[0:32, :], in1=t2[0:32, :], op=ALU.add))
                    t12_free[0] = t_a
                    t_o = b.op("act", lambda e, zs=zs, qs=qs, sl=sl: e.activation(out=qko[qs][:, sl], in_=zf[zs], func=AF.Copy),
                               waits=[t_a, qko_free[qs]])
                    t_last = t_o
                    zf_free[zs] = t_o
                    if kind == "k":
                        blk0 = (t0 + tt * 512) // MB
                        t_r = b.op("dve", lambda e, zs=zs, c=c, blk0=blk0: e.tensor_reduce(
                            out=C.kmean_f[:, c, blk0:blk0 + 2], in_=zf[zs].rearrange("p (a b) -> p a b", a=2), axis=AX.X, op=ALU.add))
                        zf_free[zs] = t_r
                dst = S["q_s"] if kind == "q" else S["k_s"]
                qko_free[qs] = b.dma("sp", dst[c * 128:(c + 1) * 128, t0:t0 + TB], qko[qs], C.misc_sem[6 + qs], waits=[t_last])
        return last

    stream(len(units), 3, load, compute)
    b.barrier()


def mix_lru(C, l, S):
    b = C.b
    A = C.arena
    T = S["T"]
    NT8 = T // 512
    o = 0
    uext = [A.at(o + s * (T + 4) * 4, F32, [T + 4]) for s in range(2)]; o += 2 * (T + 4) * 4
    uc = A.at(o, F32, [T]); o += T * 4
    r = A.at(o, F32, [T]); o += T * 4
    ii = A.at(o, F32, [T]); o += T * 4
    a = A.at(o, F32, [T]); o += T * 4
    hout = [A.at(o + s * T * 2, BF16, [T]) for s in range(2)]; o += 2 * T * 2
    wab = A.at(o, F32, [8, 128]); o += 8 * 128 * 4
    wxb = A.at(o, F32, [8, 128]); o += 8 * 128 * 4
    assert o <= A.nbytes
    t_w1 = b.dma("sp", wab, S["wa_bd"][l], C.misc_sem[0])
    t_w2 = b.dma("sp", wxb, S["wx_bd"][l], C.misc_sem[1])
    t_z = None
    for s in range(2):
        t_z = b.op("dve", lambda e, s=s: e.memset(uext[s][:, 0:4], 0.0))
    t_e = b.op("act", lambda e: e.activation(out=C.cp[:, :], in_=C.lruv[:, l, 4, :], func=AF.Exp, scale=-1.0))
    t_e = b.op("act", lambda e: e.activation(out=C.cp[:, :], in_=C.cp[:, :], func=AF.Ln, bias=C.one_col[:, 0:1]))
    b.op("dve", lambda e: e.tensor_scalar_mul(out=C.cp2[:, :], in0=C.cp[:, :], scalar1=-16.0), waits=[t_e])
    t_cp = b.op("dve", lambda e: e.tensor_scalar_mul(out=C.cp[:, :], in0=C.cp[:, :], scalar1=-8.0))
    u_free = [None, None]
    h_free = [None, None]
    ld = {}

    def load(c):
        s = c % 2
        return b.dma("sp", uext[s][:, 4:4 + T], S["u_s"][c * 128:(c + 1) * 128, 0:T], C.xs_sem[s], waits=[u_free[s], t_z])

    ld[0] = load(0)
    prev_dve = None
    for c in range(8):
        s = c % 2
        if c + 1 < 8:
            ld[c + 1] = load(c + 1)
        ue = uext[s]
        cw = lambda tap: C.lruv[:, l, tap, c:c + 1]
        t_c = b.op("act", lambda e, ue=ue, c=c: e.activation(out=uc, in_=ue[:, 4:4 + T], func=AF.Identity,
                                                             scale=C.lruv[:, l, 3, c:c + 1], bias=C.lruv[:, l, 5, c:c + 1]),
                   waits=[ld[c], prev_dve, C.lru_pe_free])
        for tap in range(3):
            off = 1 + tap
            t_c = b.op("dve", lambda e, ue=ue, tap=tap, off=off, c=c: e.scalar_tensor_tensor(
                out=uc, in0=ue[:, off:off + T], scalar=C.lruv[:, l, tap, c:c + 1], in1=uc, op0=ALU.mult, op1=ALU.add),
                waits=[t_c])
        u_free[s] = t_c
        last_act = None
        for (wt, dst, bidx, bank0) in ((wab, r, 6, 0), (wxb, ii, 7, 4)):
            for t8 in range(NT8):
                bank = bank0 + (t8 % 4)
                sl = slice(t8 * 512, (t8 + 1) * 512)
                t_m = b.op("pe", lambda e, bank=bank, wt=wt, sl=sl, c=c: e.matmul(C.ps[bank][:, :], lhsT=wt[:, c, :], rhs=uc[:, sl],
                                                                                   start=True, stop=True),
                           waits=[t_c, C.bank_free[bank], t_w1, t_w2])
                last_act = b.op("act", lambda e, bank=bank, dst=dst, sl=sl, bidx=bidx, c=c: e.activation(
                    out=dst[:, sl], in_=C.ps[bank][:, :], func=AF.Sigmoid, bias=C.lruv[:, l, bidx, c:c + 1]),
                    waits=[t_m, prev_dve])
                C.bank_free[bank] = last_act
                C.lru_pe_free = t_m
        t_a = b.op("act", lambda e, c=c: e.activation(out=a, in_=r, func=AF.Exp, scale=C.cp[:, c:c + 1]), waits=[t_cp, prev_dve])
        b.op("act", lambda e, c=c: e.activation(out=r, in_=r, func=AF.Exp, scale=C.cp2[:, c:c + 1]))
        t_s = b.op("act", lambda e: e.activation(out=r, in_=r, func=AF.Sqrt, scale=-1.0, bias=C.one_col[:, 0:1]))
        b.op("dve", lambda e: e.tensor_tensor(out=ii, in0=ii, in1=r, op=ALU.mult), waits=[t_s])
        b.op("dve", lambda e: e.tensor_tensor(out=ii, in0=ii, in1=uc, op=ALU.mult))
        hs = c % 2
        prev_dve = b.op("dve", lambda e, hs=hs: e.tensor_tensor_scan(out=hout[hs], data0=a, data1=ii, initial=0.0,
                                                                      op0=ALU.mult, op1=ALU.add), waits=[t_a, h_free[hs]])
        h_free[hs] = b.dma("sp", S["hl_s"][c * 128:(c + 1) * 128, 0:T], hout[hs], C.misc_sem[2 + hs], waits=[prev_dve])
    b.barrier()


def mix_attn(C, l, S):
    b = C.b
    A = C.arena
    T = S["T"]
    NQT = T // 128
    NB = T // MB
    o = 0
    KT = [A.at(o + s * T * 2, BF16, [T]) for s in range(2)]; o += 2 * T * 2
    QT = [A.at(o + s * T * 2, BF16, [T]) for s in range(2)]; o += 2 * T * 2
    V = [A.at(o + s * T * 2, BF16, [NQT, 128]) for s in range(2)]; o += 2 * T * 2
    ost = [A.at(o + s * T * 2, BF16, [T]) for s in range(2)]; o += 2 * T * 2
    biasT = A.at(o, BF16, [T]); o += T * 2
    gm = A.at(o, F32, [NQT, 16]); o += NQT * 64
    selt = A.at(o, F32, [NQT, 16]); o += NQT * 64
    top8 = A.at(o, F32, [NQT, 8]); o += NQT * 32
    biasq = A.at(o, BF16, [NQT, 16]); o += NQT * 32
    PT = [A.at(o + s * 1024, BF16, [512]) for s in range(4)]; o += 4096
    rsb = [A.at(o + s * 1024, F32, [256]) for s in range(2)]; o += 2048
    kmb = A.at(o, BF16, [NH, 16]); o += NH * 32
    masks = A.at(o, F32, [3, 32, 16]); o += 3 * 32 * 16 * 4
    C.mE1, C.mPB, C.mE2 = masks[:, 0], masks[:, 1], masks[:, 2]
    C.sel_b = A.at(o, BF16, [2048]); o += 4096
    C.causal_b = A.at(o, BF16, [2, 256]); o += 1024
    assert o <= A.nbytes
    scale = float(HD) ** -0.5
    t_c1 = b.dma("sp", masks, S["masks_h"], C.misc_sem[2])
    t_c2 = b.dma("pool", C.sel_b[0:16, :], S["sel_h"], C.misc_sem[3])
    t_c3 = b.dma("pool", C.causal_b, S["causal_h"], C.misc_sem[4])
    b.wait_only("pe", [t_c2, t_c3])
    b.wait_only("dve", [t_c1])
    t_km = b.op("dve", lambda e: e.tensor_scalar_mul(out=kmb, in0=C.kmean_f[:, :, :], scalar1=1.0 / MB))
    ld_sem = C.w13_sem
    h_free = [None, None]
    o_free = [None, None]

    def load(h):
        s = h % 2
        w = [h_free[s]]
        b.dma("sp", KT[s], S["k_s"][h * 128:(h + 1) * 128, 0:T], ld_sem[s], waits=w)
        b.dma("sp", QT[s], S["q_s"][h * 128:(h + 1) * 128, 0:T], ld_sem[s], waits=w)
        return b.dma("sp", V[s], S["v_s"].rearrange("(i p) c -> p i c", p=128)[:, 0:NQT, h * 128:(h + 1) * 128], ld_sem[s], waits=w)

    ld = {0: load(0)}
    pt_free = [None] * 4
    pti = 0
    rs_free = [None, None]
    accp = 0
    bias_free = None
    for h in range(NH):
        s = h % 2
        if h + 1 < NH:
            ld[h + 1] = load(h + 1)
        kt_, qt_, v_ = KT[s], QT[s], V[s]
        t_g = None
        for qt in range(NQT):
            w = [ld[h], t_km, C.bank_free[7]] if qt == 0 else []
            t_g = b.op("pe", lambda e, qt=qt, qt_=qt_, h=h: e.matmul(C.ps[7][:, qt * 16:(qt + 1) * 16], lhsT=qt_[:, qt * 128:(qt + 1) * 128],
                                                                     rhs=kmb[:, h, :], start=True, stop=True), waits=w,
                       signal=(qt == NQT - 1))
        psG = C.ps[7][:, 0:NQT * 16].rearrange("p (a b) -> p a b", b=16)
        t_1 = b.op("dve", lambda e, psG=psG: e.tensor_tensor(out=gm, in0=psG, in1=C.mE1[:, 0:NQT, :], op=ALU.add), waits=[t_g])
        C.bank_free[7] = t_1
        for qt in range(NQT):
            b.op("dve", lambda e, qt=qt: e.max(out=top8[:, qt, :], in_=gm[:, qt, :]), signal=False)
        b.op("dve", lambda e: e.tensor_tensor(out=selt, in0=gm, in1=top8[:, :, 2:3].to_broadcast([128, NQT, 16]), op=ALU.is_ge))
        b.op("dve", lambda e: e.scalar_tensor_tensor(out=selt, in0=selt, scalar=-1.0, in1=C.mPB[:, 0:NQT, :], op0=ALU.add, op1=ALU.mult))
        t_bq = b.op("dve", lambda e: e.tensor_tensor(out=biasq, in0=selt, in1=C.mE2[:, 0:NQT, :], op=ALU.add), waits=[bias_free])
        t_t = None
        for qt in range(NQT):
            bank = (qt * 128) // 1024
            col = (qt * 128) % 1024
            pv = C.ps[bank][:, :].bitcast(BF16)
            w = [t_bq, C.bank_free[bank]] if col == 0 else []
            t_t = b.op("pe", lambda e, pv=pv, col=col, qt=qt: e.transpose(out=pv[0:16, col:col + 128], in_=biasq[:, qt, :],
                                                                        identity=C.ident_b[:, :]), waits=w,
                       signal=(col == 896 or qt == NQT - 1))
            if col == 896 or qt == NQT - 1:
                t_cp = b.op("act", lambda e, pv=pv, bank=bank: e.activation(
                    out=biasT[0:16, bank * 1024:bank * 1024 + min(1024, T - bank * 1024)],
                    in_=pv[0:16, 0:min(1024, T - bank * 1024)], func=AF.Copy), waits=[t_t, bias_free])
                C.bank_free[bank] = t_cp
        t_bias = t_cp
        os_ = ost[s]
        last_pe = None
        for j in range(NB):
            ob = 4 + 2 * (accp % 2)
            rb = ob + 1
            accp += 1
            qsl = slice(j * MB, (j + 1) * MB)
            for n in range(j + 1):
                sb_ = (pti % 4)
                bk = sb_
                pti += 1
                for k2 in range(2):
                    kt = 2 * n + k2
                    csl = slice(k2 * 256, (k2 + 1) * 256)
                    w = [C.bank_free[bk], t_bias] if k2 == 0 else []
                    b.op("pe", lambda e, bk=bk, csl=csl, kt=kt, qsl=qsl, kt_=kt_, qt_=qt_: e.matmul(
                        C.ps[bk][:, csl], lhsT=kt_[:, kt * 128:(kt + 1) * 128], rhs=qt_[:, qsl], start=True, stop=False),
                        waits=w, signal=False)
                    diag = (n == j)
                    t_s = b.op("pe", lambda e, bk=bk, csl=csl, n=n, qsl=qsl, diag=diag: e.matmul(
                        C.ps[bk][:, csl], lhsT=C.sel_b[0:16, n * 128:(n + 1) * 128], rhs=biasT[0:16, qsl], start=False, stop=(not diag)),
                        signal=(not diag and k2 == 1))
                    if diag:
                        t_s = b.op("pe", lambda e, bk=bk, csl=csl, k2=k2: e.matmul(
                            C.ps[bk][:, csl], lhsT=C.ident_b[:, :], rhs=C.causal_b[:, k2, :], start=False, stop=True),
                            signal=(k2 == 1))
                pt = PT[sb_]
                t_e = b.op("act", lambda e, bk=bk, pt=pt: e.activation(out=pt, in_=C.ps[bk][:, :], func=AF.Exp, scale=scale),
                           waits=[t_s, pt_free[sb_]])
                C.bank_free[bk] = t_e
                for k2 in range(2):
                    kt = 2 * n + k2
                    csl = slice(k2 * 256, (k2 + 1) * 256)
                    first = (n == 0 and k2 == 0)
                    lastf = (n == j and k2 == 1)
                    w = [t_e] + ([C.bank_free[ob], C.bank_free[rb]] if first else [])
                    b.op("pe", lambda e, ob=ob, kt=kt, csl=csl, pt=pt, v_=v_, first=first, lastf=lastf: e.matmul(
                        C.ps[ob][:, 0:256], lhsT=v_[:, kt, :], rhs=pt[:, csl], start=first, stop=lastf), waits=w, signal=False)
                    last_pe = b.op("pe", lambda e, rb=rb, csl=csl, pt=pt, first=first, lastf=lastf: e.matmul(
                        C.ps[rb][:, 0:256], lhsT=C.ones_b[:, :], rhs=pt[:, csl], start=first, stop=lastf), signal=(k2 == 1))
                pt_free[sb_] = last_pe
            ri = j % 2
            t_r = b.op("act", lambda e, rb=rb, ri=ri: e.activation(out=rsb[ri], in_=C.ps[rb][:, 0:256], func=AF.Copy),
                       waits=[last_pe, rs_free[ri]])
            C.bank_free[rb] = t_r
            b.op("dve", lambda e, ri=ri: e.reciprocal(out=rsb[ri], in_=rsb[ri]), waits=[t_r])
            t_o = b.op("dve", lambda e, ob=ob, ri=ri, os_=os_, qsl=qsl: e.tensor_tensor(out=os_[:, qsl], in0=C.ps[ob][:, 0:256], in1=rsb[ri],
                                                                                      op=ALU.mult), waits=[last_pe, o_free[s]])
            C.bank_free[ob] = t_o
            rs_free[ri] = t_o
        bias_free = last_pe
        h_free[s] = last_pe
        o_free[s] = b.dma("sp", S["o_s"][h * 128:(h + 1) * 128, 0:T], os_, C.misc_sem[s], waits=[t_o])
    b.barrier()


def mix_out_block(C, xin, xout, t0, l, w_in, w_lu, w_au, w_out, S):
    b = C.b
    A = C.arena
    xin_v = xin.rearrange("(kc p) t -> p kc t", p=128)
    xout_v = xout.rearrange("(kc p) t -> p kc t", p=128)
    wiv = w_in.rearrange("(kc p) n -> p kc n", p=128)
    wluv = w_lu.rearrange("(kc p) n -> p kc n", p=128)
    wauv = w_au.rearrange("(kc p) n -> p kc n", p=128)
    wov = w_out.rearrange("(kc p) n -> p kc n", p=128)
    o = 0
    hn = A.at(0, BF16, [KC, TB])
    hl = A.at(32768, BF16, [8, TB])
    oa = A.at(49152, BF16, [8, TB])
    y = A.at(0, F32, [KC, TB])
    o = 65536
    mg = A.at(o, BF16, [KC, TB]); o += 32768
    wsl = []
    for s in range(2):
        wsl.append((A.at(o, BF16, [KC, 128]), A.at(o + 4096, BF16, [KC, 128]), A.at(o + 8192, BF16, [8, 128]),
                    A.at(o + 10240, BF16, [8, 128])))
        o += 12288
    wos = [A.at(o + s * 4096, BF16, [KC, 128]) for s in range(3)]; o += 3 * 4096
    xs = [A.at(o + s * 4096, F32, [TB]) for s in range(3)]; o += 3 * 4096
    rstd = A.at(o, F32, [TB]); o += 4096
    sq = [A.at(o + s * 4096, F32, [TB]) for s in range(2)]; o += 8192
    tmps = [A.at(o + s * 2048, F32, [512]) for s in range(6)]; o += 6 * 2048
    assert o <= A.nbytes

    t_hl = b.dma("sp", hl, S["hl_s"].rearrange("(c p) t -> p c t", p=128)[:, :, t0:t0 + TB], C.misc_sem[0])
    t_oa = b.dma("sp", oa, S["o_s"].rearrange("(c p) t -> p c t", p=128)[:, :, t0:t0 + TB], C.misc_sem[1])
    t_xn = emit_prenorm(C, xin_v, t0, l * 6 + 2, hn, xs, rstd, sq, psS=[6, 7])

    wo_loaded = {}

    def wo_load(m, s, waits):
        return [b.dma("pool", wos[s], wov[:, :, m * 128:(m + 1) * 128], C.w2_sem[s], waits=waits)]

    tmp_free = [None] * 6
    st = {"i": 0}

    def load(m, s, waits):
        b.dma("pool", wsl[s][0], wiv[:, :, 4096 + m * 128:4096 + (m + 1) * 128], C.w13_sem[s], waits=waits)
        b.dma("pool", wsl[s][1], wiv[:, :, 6144 + m * 128:6144 + (m + 1) * 128], C.w13_sem[s], waits=waits)
        b.dma("pool", wsl[s][2], wluv[:, :, m * 128:(m + 1) * 128], C.w13_sem[s], waits=waits)
        return [b.dma("pool", wsl[s][3], wauv[:, :, m * 128:(m + 1) * 128], C.w13_sem[s], waits=waits)]

    def compute(m, s, ld):
        last = None
        first = [True]
        for tt in range(NTT):
            base = (st["i"] % 2) * 4
            st["i"] += 1
            sl = slice(tt * 512, (tt + 1) * 512)
            toks = []
            for gi, (src, nk) in enumerate(((hn, KC), (hn, KC), (hl, 8), (oa, 8))):
                bank = base + gi
                wt = wsl[s][gi]
                for kc in range(nk):
                    w = []
                    if kc == 0:
                        w = [C.bank_free[bank]]
                        if first[0]:
                            w += list(ld) + ([t_xn[-1], t_hl, t_oa] if m == 0 else [])
                            first[0] = False
                    last = b.op("pe", lambda e, bank=bank, wt=wt, kc=kc, src=src, sl=sl: e.matmul(
                        C.ps[bank][:, :], lhsT=wt[:, kc, :], rhs=src[:, kc, sl], start=(kc == 0), stop=(kc == nk - 1)),
                        waits=w, signal=(kc == nk - 1))
                toks.append(last)
            ti = (st["i"] % 2) * 3
            sa, sb_, t1 = tmps[ti], tmps[ti + 1], tmps[ti + 2]
            t_sa = b.op("act", lambda e, base=base, sa=sa, m=m: e.activation(out=sa, in_=C.ps[base][:, :], func=AF.Sigmoid,
                                                                           bias=C.bg[:, l, m:m + 1]), waits=[toks[0], tmp_free[ti]])
            C.bank_free[base] = t_sa
            t_sb = b.op("act", lambda e, base=base, sb_=sb_, m=m: e.activation(out=sb_, in_=C.ps[base + 1][:, :], func=AF.Sigmoid,
                                                                             bias=C.bg[:, l, 16 + m:17 + m]), waits=[toks[1], tmp_free[ti + 1]])
            C.bank_free[base + 1] = t_sb
            t_1 = b.op("dve", lambda e, base=base, sa=sa, t1=t1: e.tensor_tensor(out=t1, in0=sa, in1=C.ps[base + 2][:, :], op=ALU.mult),
                       waits=[t_sa, toks[2], tmp_free[ti + 2]])
            C.bank_free[base + 2] = t_1
            tmp_free[ti] = t_1
            t_2 = b.op("dve", lambda e, base=base, sb_=sb_: e.tensor_tensor(out=sb_, in0=sb_, in1=C.ps[base + 3][:, :], op=ALU.mult),
                       waits=[t_sb, toks[3]])
            C.bank_free[base + 3] = t_2
            t_3 = b.op("dve", lambda e, t1=t1, sb_=sb_, m=m, sl=sl: e.tensor_tensor(out=mg[:, m, sl], in0=t1, in1=sb_, op=ALU.add))
            tmp_free[ti + 1] = t_3
            tmp_free[ti + 2] = t_3
            st["mg"] = t_3
        if m == 0:
            for mm in range(3):
                wo_loaded[mm] = wo_load(mm, mm % 3, [])
        return last

    stream(KC, 2, load, compute)

    psS = [6, 7]
    stat = {"last": None}
    for m in range(KC):
        s = m % 3
        ld = wo_loaded.pop(m)
        par = m % 2
        psY = [par * 2 + tt for tt in range(NTT)]
        last = None
        for kc in range(KC):
            for tt in range(NTT):
                w = []
                if kc == 0:
                    w = list(ld) + [C.bank_free[psY[tt]]] + ([st["mg"]] if m == 0 else [])
                last = b.op("pe", lambda e, tt=tt, kc=kc, s=s, bank=psY[tt]: e.matmul(
                    C.ps[bank][:, :], lhsT=wos[s][:, kc, :], rhs=mg[:, kc, tt * 512:(tt + 1) * 512],
                    start=(kc == 0), stop=(kc == KC - 1)), waits=w, signal=(kc == KC - 1))
        stat["last"] = emit_outproj_stats(C, m, psY, psS, y, sq, last)
        if m + 3 < KC:
            wo_loaded[m + 3] = wo_load(m + 3, s, [last])
    t_r = emit_stats_finish(C, psS, rstd, [stat["last"]])
    emit_post_residual(C, y, l * 6 + 3, rstd, t_r, xin_v, xout_v, t0, xs, 1.0)
    b.barrier()


def build_program(T=SEQ, phases=None, NL=DEPTH):
    if phases is None:
        phases = []
        for l in range(NL):
            phases += ["ffn%d_0" % l, "mix%d" % l, "ffn%d_1" % l]
    nc = bass.Bass("TRN2", target_bir_lowering=False)
    C = Ctx()
    C.nc = nc
    dt = nc.dram_tensor

    def ext(name, shape):
        return dt(name, list(shape), F32, kind="ExternalInput").ap()

    xT = ext("xT", [D, T])
    ng_h = ext("ng_h", [128, NL * 6, KC])
    w13 = ext("ffn_w13", [NL, 2, D, 2 * DFF])
    w2 = ext("ffn_w2", [NL, 2, DFF, D])
    w_in = ext("w_in", [NL, D, IN_COLS])
    w_lu = ext("w_lru_up", [NL, D_LRU, D])
    w_au = ext("w_attn_up", [NL, D_ATTN, D])
    w_out = ext("w_out", [NL, D, D])
    bg_h = ext("bg_h", [128, NL, 32])
    lruv_h = ext("lruv_h", [128, NL, 8, 8])
    wa_bd = ext("wa_bd", [NL, 128, 8, 128])
    wx_bd = ext("wx_bd", [NL, 128, 8, 128])
    ropeC = ext("ropeC", [32, T])
    ropeS = ext("ropeS", [32, T])
    perm_h = ext("perm_h", [32, 32])
    masks_h = ext("masks_h", [128, 3, 32, 16])
    sel_h = ext("sel_h", [16, 2048])
    causal_h = ext("causal_h", [128, 2, 256])
    ident_h = ext("ident_h", [128, 128])
    outT = dt("outT", [D, T], F32, kind="ExternalOutput").ap()
    xres = dt("xres", [D, T], F32).ap()
    S = {"masks_h": masks_h, "sel_h": sel_h, "causal_h": causal_h, "T": T, "ropeC": ropeC, "ropeS": ropeS, "wa_bd": wa_bd, "wx_bd": wx_bd,
         "u_s": dt("u_s", [D_LRU, T], F32).ap(), "q_s": dt("q_s", [D_ATTN, T], BF16).ap(),
         "k_s": dt("k_s", [D_ATTN, T], BF16).ap(), "v_s": dt("v_s", [T, D_ATTN], BF16).ap(),
         "hl_s": dt("hl_s", [D_LRU, T], BF16).ap(), "o_s": dt("o_s", [D_ATTN, T], BF16).ap()}

    b = Builder(nc)
    C.b = b
    global LAST_B
    LAST_B = b
    es = contextlib.ExitStack()
    with es:
        ARENA_BYTES = 202752
        sb = lambda name, shape, dtp: es.enter_context(nc.sbuf_tensor(name, shape, dtp))
        ah = sb("arena", [128, ARENA_BYTES // 2], BF16)
        C.arena = Arena(ah, ARENA_BYTES)
        C.ng = sb("ng", [128, NL * 6, KC], F32)
        C.bg = sb("bg", [128, NL, 32], F32)
        C.lruv = sb("lruv", [128, NL, 8, 8], F32)
        C.cp = sb("cp", [128, 8], F32)
        C.cp2 = sb("cp2", [128, 8], F32)
        C.kmean_f = sb("kmean_f", [128, NH, 16], F32)
        C.ones_f = sb("ones_f", [128, 128], F32)
        C.ones_b = sb("ones_b", [128, 128], BF16)
        C.ident_b = sb("ident_b", [128, 128], BF16)
        C.perm = sb("perm", [32, 32], F32)
        C.eps = sb("eps", [128, 1], F32)
        C.one_col = sb("one_col", [128, 1], F32)
        C.ps = [es.enter_context(nc.psum_tensor("ps%d" % i, [128, 512], F32)) for i in range(8)]
        C.bank_free = [None] * 8
        C.xs_sem = [b.new_dsem() for _ in range(3)]
        C.xs_free = [None] * 3
        C.sq_free = [None] * 2
        C.w13_sem = [b.new_dsem() for _ in range(3)]
        C.w2_sem = [b.new_dsem() for _ in range(3)]
        C.misc_sem = [b.new_dsem() for _ in range(8)]
        C.lru_pe_free = None
        csem = b.new_dsem()

        toks = [b.dma("sp", C.ng[:, :, :], ng_h, csem), b.dma("sp", C.bg[:, :, :], bg_h, csem),
                b.dma("sp", C.lruv[:, :, :, :], lruv_h, csem), b.dma("sp", C.perm[:, :], perm_h, csem),
                b.dma("pool", C.ident_b[:, :], ident_h, csem)]
        b.op("dve", lambda e: e.memset(C.ones_f[:, :], 1.0))
        b.op("dve", lambda e: e.memset(C.ones_b[:, :], 1.0))
        b.op("dve", lambda e: e.memset(C.one_col[:, :], 1.0))
        b.op("dve", lambda e: e.memset(C.kmean_f[:, :, :], 0.0))
        b.op("dve", lambda e: e.memset(C.eps[:, :], EPS), waits=toks)
        b.barrier()

        cur = xT
        plist = list(phases)
        for pi, ph in enumerate(plist):
            dst = outT if pi == len(plist) - 1 else xres
            head = ph.split("_")[0]
            kind, l = head[:3], int(head[3:])
            if kind == "ffn":
                i = int(ph.split("_")[1])
                for tb in range(T // TB):
                    ffn_block(C, cur, dst, tb * TB, l * 6 + (0 if i == 0 else 4), l * 6 + (1 if i == 0 else 5),
                              w13[l, i], w2[l, i])
            elif kind == "mix":
                for tb in range(T // TB):
                    mix_inproj_block(C, cur, tb * TB, l, w_in[l], S)
                mix_lru(C, l, S)
                mix_attn(C, l, S)
                for tb in range(T // TB):
                    mix_out_block(C, cur, dst, tb * TB, l, w_in[l], w_lu[l], w_au[l], w_out[l], S)
            cur = dst
        b.barrier()
        b.emit()
    return nc


def host_consts(T):
    pos = np.arange(T, dtype=np.float32)
    inv = (np.float32(500000.0) ** (-np.arange(0, 32, 2, dtype=np.float32) / np.float32(32))).astype(np.float32)
    ang = (pos[:, None] * inv[None, :]).astype(np.float32)
    cos = np.cos(ang).astype(np.float32).T
    sin = np.sin(ang).astype(np.float32).T
    ropeC = np.ascontiguousarray(np.concatenate([cos, cos], axis=0))
    ropeS = np.ascontiguousarray(np.concatenate([-sin, sin], axis=0))
    perm = np.zeros((32, 32), np.float32)
    for dst in range(32):
        perm[(dst + 16) % 32, dst] = 1.0
    masks = np.zeros((128, 3, 32, 16), np.float32)
    for qt in range(32):
        j = qt // 2
        for n in range(16):
            masks[:, 0, qt, n] = 0.0 if n < j else -1e30
            masks[:, 1, qt, n] = -NEG if n < j else 0.0
            masks[:, 2, qt, n] = 0.0 if n <= j else NEG
    sel = np.zeros((16, 16, 128), np.float32)
    for n in range(16):
        sel[n, n, :] = 1.0
    causal = np.zeros((128, 2, 256), np.float32)
    for k2 in range(2):
        kk = k2 * 128 + np.arange(128)[:, None]
        qq = np.arange(256)[None, :]
        causal[:, k2, :] = np.where(kk > qq, NEG, 0.0)
    return {"ropeC": ropeC, "ropeS": ropeS, "perm_h": perm, "masks_h": masks, "sel_h": sel.reshape(16, 2048),
            "causal_h": causal, "ident_h": np.eye(128, dtype=np.float32)}


def host_layout(inputs, T=SEQ, ncores=BATCH, l0=0, NL=DEPTH, x_T=None):
    f = lambda k: np.asarray(inputs[k], dtype=np.float32)[l0:l0 + NL]
    ng_h = np.ascontiguousarray(f("norm_gains").reshape(NL * 6, KC, 128).transpose(2, 0, 1))
    bg_h = np.ascontiguousarray(f("b_gate").reshape(NL, 32, 128).transpose(2, 0, 1))
    lruv = np.zeros((128, NL, 8, 8), np.float32)
    lruv[:, :, 0:4, :] = f("conv_w").reshape(NL, 4, 8, 128).transpose(3, 0, 1, 2)
    lruv[:, :, 4, :] = f("lru_lambda").reshape(NL, 8, 128).transpose(2, 0, 1)
    lruv[:, :, 5, :] = f("conv_b").reshape(NL, 8, 128).transpose(2, 0, 1)
    lruv[:, :, 6, :] = f("lru_ba").reshape(NL, 8, 128).transpose(2, 0, 1)
    lruv[:, :, 7, :] = f("lru_bx").reshape(NL, 8, 128).transpose(2, 0, 1)

    def blockdiag(w):
        o = np.zeros((NL, 128, 8, 128), np.float32)
        for c in range(8):
            o[:, 0:64, c, 0:64] = w[:, 2 * c]
            o[:, 64:128, c, 64:128] = w[:, 2 * c + 1]
        return o

    common = {"ng_h": ng_h, "bg_h": bg_h, "lruv_h": lruv, "wa_bd": blockdiag(f("lru_wa")), "wx_bd": blockdiag(f("lru_wx")),
              "ffn_w13": f("ffn_w13"), "ffn_w2": f("ffn_w2"), "w_in": f("w_in"), "w_lru_up": f("w_lru_up"),
              "w_attn_up": f("w_attn_up"), "w_out": f("w_out")}
    common.update(host_consts(T))
    maps = []
    for c in range(ncores):
        m = dict(common)
        if x_T is not None:
            m["xT"] = x_T[c]
        else:
            m["xT"] = np.ascontiguousarray(np.asarray(inputs["x"], dtype=np.float32)[c, :T].T)
        maps.append(m)
    return maps


N_LAUNCH_LAYERS = 1


def kernel(**inputs):
    NL = N_LAUNCH_LAYERS
    nc = build_program(SEQ, None, NL)
    x_T = None
    for l0 in range(0, DEPTH, NL):
        maps = host_layout(inputs, SEQ, BATCH, l0, NL, x_T)
        res = run_bass_kernel_spmd(nc, maps, core_ids=list(range(BATCH)))
        x_T = [res.results[c]["outT"] for c in range(BATCH)]
    out = np.stack([np.ascontiguousarray(x_T[c].T) for c in range(BATCH)], axis=0)
    return out.astype(np.float32)
```

```python
import contextlib
import numpy as np
import concourse.bass as bass
import concourse.mybir as mybir
from concourse.bass_utils import run_bass_kernel_spmd

F32 = mybir.dt.float32
BF16 = mybir.dt.bfloat16
AF = mybir.ActivationFunctionType
ALU = mybir.AluOpType
AX = mybir.AxisListType

D = 2048
KC = 16
DFF = 5632
FC = 44
DEPTH = 2
SEQ = 4096
BATCH = 4
D_LRU = 1024
D_ATTN = 1024
NH = 8
HD = 128
MB = 256
NBLK = SEQ // MB
IN_COLS = 8192
EPS = 1e-6
TB = 1024
NTT = TB // 512
NEG = -30000.0


class _Eng:
    def __init__(self, name):
        self.name = name
        self.items = []
        self.sem = None
        self.count = 0
        self.waited = {}
        self.last = None


class Builder:
    ENG = ("pe", "act", "dve", "pool", "sp")

    def __init__(self, nc):
        self.nc = nc
        self.engs = {n: _Eng(n) for n in self.ENG}
        for n in self.ENG:
            self.engs[n].sem = nc.alloc_semaphore("prog_" + n)
        self.nsem = 0
        self.dma_pending = {}
        self.swdge_fifo = []
        self.swdge_out = 0

    def _waits(self, e, waits):
        ws = []
        for t in waits:
            if t is None:
                continue
            sem, val, owner = t
            if owner == e.name:
                continue
            key = id(sem)
            if e.waited.get(key, 0) >= val:
                continue
            e.waited[key] = val
            ws.append((sem, val))
        return ws

    def op(self, eng, fn, waits=(), signal=True):
        e = self.engs[eng]
        ws = self._waits(e, waits)
        tok = None
        inc = None
        if signal:
            e.count += 1
            tok = (e.sem, e.count, eng)
            inc = (e.sem, 1)
            e.last = tok
        e.items.append((fn, ws, inc))
        return tok

    def new_dsem(self, name=None):
        self.nsem += 1
        h = self.nc.alloc_semaphore(name or ("dsem%d" % self.nsem))
        return [h, 0]

    def dma(self, queue, out, in_, dsem, waits=(), **kw):
        e = self.engs[queue]
        waits = list(waits)
        if queue == "pool":
            nd = 1
            for d_ in out.shape[:-1]:
                nd *= d_
            nd = (nd + 15) // 16 + 2
            while self.swdge_fifo and self.swdge_out + nd > 640:
                t_old, n_old = self.swdge_fifo.pop(0)
                self.swdge_out -= n_old
                waits.append(t_old)
        ws = self._waits(e, waits)
        dsem[1] += 16
        tok = (dsem[0], dsem[1], "dma")
        if queue == "pool":
            self.swdge_fifo.append((tok, nd))
            self.swdge_out += nd
        self.dma_pending[id(dsem[0])] = tok
        e.items.append((lambda eng, o=out, i=in_, k=kw: eng.dma_start(out=o, in_=i, **k), ws, (dsem[0], 16)))
        return tok

    def wait_only(self, eng, waits):
        e = self.engs[eng]
        ws = self._waits(e, waits)
        if ws:
            e.items.append((None, ws, None))

    def barrier(self):
        toks = [self.engs[n].last for n in self.ENG] + list(self.dma_pending.values())
        self.dma_pending = {}
        for n in self.ENG:
            self.wait_only(n, toks)

    def emit(self):
        nc = self.nc
        engs = self.engs

        def replay(e, engobj):
            for fn, ws, inc in e.items:
                for sem, val in ws:
                    engobj.wait_ge(sem, val)
                if fn is None:
                    continue
                ins = fn(engobj)
                if inc is not None:
                    ins.then_inc(inc[0], inc[1])

        with nc.Block() as block:
            @block.tensor
            def _(pe):
                replay(engs["pe"], pe)

            @block.scalar
            def _(act):
                replay(engs["act"], act)

            @block.vector
            def _(dve):
                replay(engs["dve"], dve)

            @block.gpsimd
            def _(pool):
                replay(engs["pool"], pool)

            @block.sync
            def _(sp):
                replay(engs["sp"], sp)


class Arena:
    def __init__(self, handle, nbytes):
        self.h = handle
        self.nbytes = nbytes

    def at(self, off, dtype, shape):
        esz = 4 if dtype == F32 else 2
        n = 1
        for s in shape:
            n *= s
        nb = n * esz
        assert off % 4 == 0 and off + nb <= self.nbytes, (off, nb, self.nbytes)
        v = self.h[:, off // 2: (off + nb) // 2]
        if dtype != BF16:
            v = v.bitcast(dtype)
        if len(shape) == 2:
            return v.rearrange("p (a b) -> p a b", a=shape[0])
        if len(shape) == 3:
            return v.rearrange("p (a b c) -> p a b c", a=shape[0], b=shape[1])
        return v


class Ctx:
    pass


def stream(n, nslots, load_fn, compute_fn, first_waits=()):
    loaded = {}
    for u in range(min(nslots, n)):
        loaded[u] = load_fn(u, u % nslots, list(first_waits))
    for u in range(n):
        done = compute_fn(u, u % nslots, loaded.pop(u))
        nxt = u + nslots
        if nxt < n:
            loaded[nxt] = load_fn(nxt, nxt % nslots, [done])


def bank_wait(C, i):
    return C.bank_free[i]


def emit_stats_finish(C, psS, rstd, waits):
    b = C.b
    toks = []
    for tt in range(NTT):
        sl = slice(tt * 512, (tt + 1) * 512)
        t1 = b.op("act", lambda e, tt=tt, sl=sl: e.activation(out=rstd[:, sl], in_=C.ps[psS[tt]][:, :], func=AF.Sqrt,
                                                              scale=1.0 / D, bias=C.eps[:, 0:1]), waits=waits)
        C.bank_free[psS[tt]] = t1
        t2 = b.op("dve", lambda e, sl=sl: e.reciprocal(out=rstd[:, sl], in_=rstd[:, sl]), waits=[t1])
        toks.append(t2)
    return toks


def emit_prenorm(C, xin_v, t0, gidx, xn, xs, rstd, sq, psS):
    b = C.b
    nxs = len(xs)
    last_mm = None
    for kc in range(KC):
        s = kc % nxs
        t_ld = b.dma("sp", xs[s], xin_v[:, kc, t0:t0 + TB], C.xs_sem[s], waits=[C.xs_free[s]])
        q = kc % 2
        t_sq = b.op("act", lambda e, s=s, q=q: e.activation(out=sq[q], in_=xs[s], func=AF.Square),
                    waits=[t_ld, C.sq_free[q]])
        C.xs_free[s] = t_sq
        for tt in range(NTT):
            w = [t_sq]
            if kc == 0:
                w.append(C.bank_free[psS[tt]])
            last_mm = b.op("pe", lambda e, tt=tt, q=q, kc=kc: e.matmul(
                C.ps[psS[tt]][:, :], lhsT=C.ones_f[:, :], rhs=sq[q][:, tt * 512:(tt + 1) * 512],
                start=(kc == 0), stop=(kc == KC - 1)), waits=w, signal=True)
        C.sq_free[q] = last_mm
    t_r = emit_stats_finish(C, psS, rstd, [last_mm])
    toks = []
    for kc in range(KC):
        s = kc % nxs
        t_ld = b.dma("sp", xs[s], xin_v[:, kc, t0:t0 + TB], C.xs_sem[s], waits=[C.xs_free[s]])
        t_n = b.op("dve", lambda e, s=s, kc=kc: e.scalar_tensor_tensor(
            out=xn[:, kc, :], in0=xs[s], scalar=C.ng[:, gidx, kc:kc + 1], in1=rstd, op0=ALU.mult, op1=ALU.mult),
            waits=[t_ld] + t_r)
        C.xs_free[s] = t_n
        toks.append(t_n)
    return toks


def emit_post_residual(C, y, gidx, rstd, t_rstd, xin_v, xout_v, t0, xs, rscale):
    b = C.b
    nxs = len(xs)
    for m in range(KC):
        s = m % nxs
        t_ld = b.dma("sp", xs[s], xin_v[:, m, t0:t0 + TB], C.xs_sem[s], waits=[C.xs_free[s]])
        t_a = b.op("dve", lambda e, m=m: e.scalar_tensor_tensor(
            out=y[:, m, :], in0=y[:, m, :], scalar=C.ng[:, gidx, m:m + 1], in1=rstd, op0=ALU.mult, op1=ALU.mult),
            waits=t_rstd)
        t_b = b.op("dve", lambda e, m=m, s=s: e.scalar_tensor_tensor(
            out=xs[s], in0=y[:, m, :], scalar=float(rscale), in1=xs[s], op0=ALU.mult, op1=ALU.add), waits=[t_ld])
        t_st = b.dma("sp", xout_v[:, m, t0:t0 + TB], xs[s], C.xs_sem[s], waits=[t_b])
        C.xs_free[s] = t_st


def emit_outproj_stats(C, m, psY, psS, y, sq, t_mm):
    b = C.b
    last = None
    for tt in range(NTT):
        sl = slice(tt * 512, (tt + 1) * 512)
        t_c = b.op("act", lambda e, tt=tt, sl=sl, m=m: e.activation(out=y[:, m, sl], in_=C.ps[psY[tt]][:, :], func=AF.Copy),
                   waits=[t_mm])
        C.bank_free[psY[tt]] = t_c
        q = tt
        t_q = b.op("dve", lambda e, sl=sl, q=q, m=m: e.tensor_tensor(out=sq[q][:, 0:512], in0=y[:, m, sl],
                                                                   in1=y[:, m, sl], op=ALU.mult),
                   waits=[t_c, C.sq_free[q]])
        w = [t_q]
        if m == 0:
            w.append(C.bank_free[psS[tt]])
        last = b.op("pe", lambda e, tt=tt, q=q, m=m: e.matmul(
            C.ps[psS[tt]][:, :], lhsT=C.ones_f[:, :], rhs=sq[q][:, 0:512], start=(m == 0), stop=(m == KC - 1)), waits=w)
        C.sq_free[q] = last
    return last


def ffn_block(C, xin, xout, t0, gpre, gpost, w13, w2):
    b = C.b
    A = C.arena
    xin_v = xin.rearrange("(kc p) t -> p kc t", p=128)
    xout_v = xout.rearrange("(kc p) t -> p kc t", p=128)
    w13v = w13.rearrange("(kc p) n -> p kc n", p=128)
    w2v = w2.rearrange("(fc p) n -> p fc n", p=128)
    xn = A.at(0, BF16, [KC, TB])
    y = A.at(0, F32, [KC, TB])
    w13s = [(A.at(32768 + s * 16384, BF16, [KC, 256]), A.at(32768 + s * 16384 + 8192, BF16, [KC, 256])) for s in range(2)]
    g = A.at(65536, BF16, [FC, TB])
    o = 65536 + FC * TB * 2
    w2s = [A.at(o + s * 11264, BF16, [FC, 128]) for s in range(2)]
    o += 2 * 11264
    xs = [A.at(o + s * 4096, F32, [TB]) for s in range(3)]
    o += 3 * 4096
    rstd = A.at(o, F32, [TB])
    o += 4096
    sq = [A.at(o + s * 4096, F32, [TB]) for s in range(2)]
    tmp = [A.at(o + s * 2048, F32, [512]) for s in range(4)]
    o += 8192
    assert o <= A.nbytes, o

    t_xn = emit_prenorm(C, xin_v, t0, gpre, xn, xs, rstd, sq, psS=[6, 7])

    w2_loaded = {}

    def w2_load(m, s, waits):
        return [b.dma("pool", w2s[s], w2v[:, :, m * 128:(m + 1) * 128], C.w2_sem[s], waits=waits)]

    tmp_free = [None] * 4
    state = {"tmpi": 0}

    def a_load(u, s, waits):
        b.dma("pool", w13s[s][0], w13v[:, :, u * 256:(u + 1) * 256], C.w13_sem[s], waits=waits)
        t = b.dma("pool", w13s[s][1], w13v[:, :, DFF + u * 256:DFF + (u + 1) * 256], C.w13_sem[s], waits=waits)
        return [t]

    def a_compute(u, s, ld):
        last = None
        for j in range(2):
            c = u * 2 + j
            base = (c % 2) * 4
            toks_ab = []
            for wi in range(2):
                wt = w13s[s][wi]
                for kc in range(KC):
                    for tt in range(NTT):
                        bank = base + wi * 2 + tt
                        w = []
                        if kc == 0:
                            w = list(ld) + [C.bank_free[bank]] + ([t_xn[-1]] if (u == 0 and j == 0 and wi == 0) else [])
                        last = b.op("pe", lambda e, bank=bank, wt=wt, kc=kc, tt=tt, j=j: e.matmul(
                            C.ps[bank][:, :], lhsT=wt[:, kc, j * 128:(j + 1) * 128], rhs=xn[:, kc, tt * 512:(tt + 1) * 512],
                            start=(kc == 0), stop=(kc == KC - 1)), waits=w, signal=(kc == KC - 1))
                toks_ab.append(last)
            for tt in range(NTT):
                ti = state["tmpi"] % 4
                state["tmpi"] += 1
                bankA = base + tt
                bankB = base + 2 + tt
                t_s = b.op("act", lambda e, ti=ti, bankA=bankA: e.activation(out=tmp[ti], in_=C.ps[bankA][:, :], func=AF.Silu),
                           waits=[toks_ab[0], tmp_free[ti]])
                C.bank_free[bankA] = t_s
                t_g = b.op("dve", lambda e, ti=ti, bankB=bankB, c=c, tt=tt: e.tensor_tensor(
                    out=g[:, c, tt * 512:(tt + 1) * 512], in0=tmp[ti], in1=C.ps[bankB][:, :], op=ALU.mult),
                    waits=[t_s, toks_ab[1]])
                tmp_free[ti] = t_g
                C.bank_free[bankB] = t_g
        if u == 0:
            for m in range(2):
                w2_loaded[m] = w2_load(m, m % 2, [])
        return last

    stream(DFF // 256, 2, a_load, a_compute)

    psS = [6, 7]
    stat = {"last": None}

    def b_compute(m, s, ld):
        par = m % 2
        psY = [par * 2 + tt for tt in range(NTT)]
        last = None
        for fc in range(FC):
            for tt in range(NTT):
                w = []
                if fc == 0:
                    w = list(ld) + [C.bank_free[psY[tt]]]
                last = b.op("pe", lambda e, tt=tt, fc=fc, s=s, bank=psY[tt]: e.matmul(
                    C.ps[bank][:, :], lhsT=w2s[s][:, fc, :], rhs=g[:, fc, tt * 512:(tt + 1) * 512],
                    start=(fc == 0), stop=(fc == FC - 1)), waits=w, signal=(fc == FC - 1))
        stat["last"] = emit_outproj_stats(C, m, psY, psS, y, sq, last)
        return last

    for m in range(KC):
        s = m % 2
        ld = w2_loaded.pop(m)
        done = b_compute(m, s, ld)
        if m + 2 < KC:
            w2_loaded[m + 2] = w2_load(m + 2, s, [done])

    t_r = emit_stats_finish(C, psS, rstd, [stat["last"]])
    emit_post_residual(C, y, gpost, rstd, t_r, xin_v, xout_v, t0, xs, 0.5)
    b.barrier()


def mix_inproj_block(C, xin, t0, l, w_in, S):
    b = C.b
    A = C.arena
    T = S["T"]
    xin_v = xin.rearrange("(kc p) t -> p kc t", p=128)
    wv = w_in.rearrange("(kc p) n -> p kc n", p=128)
    o = 0
    hn = A.at(o, BF16, [KC, TB]); o += KC * TB * 2
    ws = [A.at(o + s * 8192, BF16, [KC, 256]) for s in range(3)]; o += 3 * 8192
    xs = [A.at(o + s * 4096, F32, [TB]) for s in range(3)]; o += 3 * 4096
    rstd = A.at(o, F32, [TB]); o += 4096
    sq = [A.at(o + s * 4096, F32, [TB]) for s in range(2)]; o += 8192
    ust = [A.at(o + s * 4096, F32, [TB]) for s in range(2)]; o += 8192
    zf = [A.at(o + s * 2048, F32, [512]) for s in range(2)]; o += 4096
    t1 = A.at(o, F32, [512]); o += 2048
    t2 = A.at(o, F32, [512]); o += 2048
    qko = [A.at(o + s * 2048, BF16, [TB]) for s in range(2)]; o += 4096
    vst = [A.at(o + s * 4096, BF16, [8, 256]) for s in range(2)]; o += 8192
    rC = A.at(o, F32, [TB]); o += 4096
    rS = A.at(o, F32, [TB]); o += 4096
    assert o <= A.nbytes

    t_rc = b.dma("sp", rC[0:32, :], S["ropeC"][:, t0:t0 + TB], C.misc_sem[0])
    t_rs = b.dma("sp", rS[0:32, :], S["ropeS"][:, t0:t0 + TB], C.misc_sem[1])
    t_xn = emit_prenorm(C, xin_v, t0, l * 6 + 2, hn, xs, rstd, sq, psS=[6, 7])

    units = [("u", 0 + i * 256, i) for i in range(4)] + [("k", 2048 + i * 256, i) for i in range(4)] + \
            [("v", 3072 + i * 256, i) for i in range(4)] + [("q", 1024 + i * 256, i) for i in range(4)]
    st = {"ust": 0, "zf": 0, "qko": 0, "vst": 0, "vbank": 0, "par": 0}
    ust_free = [None, None]
    zf_free = [None, None]
    qko_free = [None, None]
    vst_free = [None, None]
    t12_free = [None]

    def load(u, s, waits):
        kind, c0, _ = units[u]
        return [b.dma("pool", ws[s], wv[:, :, c0:c0 + 256], C.w13_sem[s], waits=waits)]

    def compute(u, s, ld):
        kind, c0, ui = units[u]
        wt = ws[s]
        last = None
        first = [True]

        def fw(bank):
            w = [C.bank_free[bank]]
            if first[0]:
                w += list(ld)
                if u == 0:
                    w.append(t_xn[-1])
                first[0] = False
            return w

        if kind == "v":
            vs = st["vst"] % 2
            st["vst"] += 1
            for i in range(TB // 128):
                bank = 4 + (st["vbank"] % 2)
                st["vbank"] += 1
                for kc in range(KC):
                    w = fw(bank) if kc == 0 else []
                    if kc == 0 and i == 0:
                        w.append(vst_free[vs])
                    last = b.op("pe", lambda e, bank=bank, kc=kc, i=i: e.matmul(
                        C.ps[bank][:, 0:256], lhsT=hn[:, kc, i * 128:(i + 1) * 128], rhs=wt[:, kc, :],
                        start=(kc == 0), stop=(kc == KC - 1)), waits=w, signal=(kc == KC - 1))
                t_c = b.op("act", lambda e, bank=bank, i=i, vs=vs: e.activation(out=vst[vs][:, i, :], in_=C.ps[bank][:, 0:256],
                                                                              func=AF.Copy), waits=[last, vst_free[vs]])
                C.bank_free[bank] = t_c
            vdst = S["v_s"].rearrange("(i p) c -> p i c", p=128)[:, t0 // 128:(t0 + TB) // 128, ui * 256:(ui + 1) * 256]
            vst_free[vs] = b.dma("sp", vdst, vst[vs], C.misc_sem[2 + vs], waits=[t_c])
            return last

        for j in range(2):
            c = ui * 2 + j
            par = st["par"] % 2
            st["par"] += 1
            banks = [par * 2 + tt for tt in range(NTT)]
            for kc in range(KC):
                for tt in range(NTT):
                    w = fw(banks[tt]) if kc == 0 else []
                    last = b.op("pe", lambda e, bank=banks[tt], kc=kc, tt=tt, j=j: e.matmul(
                        C.ps[bank][:, :], lhsT=wt[:, kc, j * 128:(j + 1) * 128], rhs=hn[:, kc, tt * 512:(tt + 1) * 512],
                        start=(kc == 0), stop=(kc == KC - 1)), waits=w, signal=(kc == KC - 1))
            if kind == "u":
                us = st["ust"] % 2
                st["ust"] += 1
                for tt in range(NTT):
                    t_c = b.op("act", lambda e, bank=banks[tt], tt=tt, us=us: e.activation(
                        out=ust[us][:, tt * 512:(tt + 1) * 512], in_=C.ps[bank][:, :], func=AF.Copy),
                        waits=[last, ust_free[us]])
                    C.bank_free[banks[tt]] = t_c
                ust_free[us] = b.dma("sp", S["u_s"][c * 128:(c + 1) * 128, t0:t0 + TB], ust[us], C.misc_sem[4 + us], waits=[t_c])
            else:
                qs = st["qko"] % 2
                st["qko"] += 1
                t_last = None
                for tt in range(NTT):
                    zs = st["zf"] % 2
                    st["zf"] += 1
                    sl = slice(tt * 512, (tt + 1) * 512)
                    t_c = b.op("act", lambda e, bank=banks[tt], zs=zs: e.activation(out=zf[zs], in_=C.ps[bank][:, :], func=AF.Copy),
                               waits=[last, zf_free[zs]])
                    C.bank_free[banks[tt]] = t_c
                    rb = 4 + (st["vbank"] % 2)
                    st["vbank"] += 1
                    t_p = b.op("pe", lambda e, rb=rb, zs=zs: e.matmul(C.ps[rb][0:32, :], lhsT=C.perm[0:32, 0:32], rhs=zf[zs][0:32, :],
                                                                        start=True, stop=True), waits=[t_c, C.bank_free[rb]])
                    b.op("dve", lambda e, zs=zs, sl=sl: e.tensor_tensor(out=t1[0:32, :], in0=zf[zs][0:32, :], in1=rC[0:32, sl], op=ALU.mult),
                         waits=[t_c, t_rc, t12_free[0]])
                    t_2 = b.op("dve", lambda e, rb=rb, sl=sl: e.tensor_tensor(out=t2[0:32, :], in0=C.ps[rb][0:32, :], in1=rS[0:32, sl], op=ALU.mult),
                               waits=[t_p, t_rs])
                    C.bank_free[rb] = t_2
                    t_a = b.op("dve", lambda e, zs=zs: e.tensor_tensor(out=zf[zs][0:32, :], in0=t1[0:32, :], in1=t2[0:32, :], op=ALU.add))
                    t12_free[0] = t_a
                    t_o = b.op("act", lambda e, zs=zs, qs=qs, sl=sl: e.activation(out=qko[qs][:, sl], in_=zf[zs], func=AF.Copy),
                               waits=[t_a, qko_free[qs]])
                    t_last = t_o
                    zf_free[zs] = t_o
                    if kind == "k":
                        blk0 = (t0 + tt * 512) // MB
                        t_r = b.op("dve", lambda e, zs=zs, c=c, blk0=blk0: e.tensor_reduce(
                            out=C.kmean_f[:, c, blk0:blk0 + 2], in_=zf[zs].rearrange("p (a b) -> p a b", a=2), axis=AX.X, op=ALU.add))
                        zf_free[zs] = t_r
                dst = S["q_s"] if kind == "q" else S["k_s"]
                qko_free[qs] = b.dma("sp", dst[c * 128:(c + 1) * 128, t0:t0 + TB], qko[qs], C.misc_sem[6 + qs], waits=[t_last])
        return last

    stream(len(units), 3, load, compute)
    b.barrier()


def mix_lru(C, l, S):
    b = C.b
    A = C.arena
    T = S["T"]
    NT8 = T // 512
    o = 0
    uext = [A.at(o + s * (T + 4) * 4, F32, [T + 4]) for s in range(2)]; o += 2 * (T + 4) * 4
    uc = A.at(o, F32, [T]); o += T * 4
    r = A.at(o, F32, [T]); o += T * 4
    ii = A.at(o, F32, [T]); o += T * 4
    a = A.at(o, F32, [T]); o += T * 4
    hout = [A.at(o + s * T * 2, BF16, [T]) for s in range(2)]; o += 2 * T * 2
    wab = A.at(o, F32, [8, 128]); o += 8 * 128 * 4
    wxb = A.at(o, F32, [8, 128]); o += 8 * 128 * 4
    assert o <= A.nbytes
    t_w1 = b.dma("sp", wab, S["wa_bd"][l], C.misc_sem[0])
    t_w2 = b.dma("sp", wxb, S["wx_bd"][l], C.misc_sem[1])
    t_z = None
    for s in range(2):
        t_z = b.op("dve", lambda e, s=s: e.memset(uext[s][:, 0:4], 0.0))
    t_e = b.op("act", lambda e: e.activation(out=C.cp[:, :], in_=C.lruv[:, l, 4, :], func=AF.Exp, scale=-1.0))
    t_e = b.op("act", lambda e: e.activation(out=C.cp[:, :], in_=C.cp[:, :], func=AF.Ln, bias=C.one_col[:, 0:1]))
    b.op("dve", lambda e: e.tensor_scalar_mul(out=C.cp2[:, :], in0=C.cp[:, :], scalar1=-16.0), waits=[t_e])
    t_cp = b.op("dve", lambda e: e.tensor_scalar_mul(out=C.cp[:, :], in0=C.cp[:, :], scalar1=-8.0))
    u_free = [None, None]
    h_free = [None, None]
    ld = {}

    def load(c):
        s = c % 2
        return b.dma("sp", uext[s][:, 4:4 + T], S["u_s"][c * 128:(c + 1) * 128, 0:T], C.xs_sem[s], waits=[u_free[s], t_z])

    ld[0] = load(0)
    prev_dve = None
    for c in range(8):
        s = c % 2
        if c + 1 < 8:
            ld[c + 1] = load(c + 1)
        ue = uext[s]
        cw = lambda tap: C.lruv[:, l, tap, c:c + 1]
        t_c = b.op("act", lambda e, ue=ue, c=c: e.activation(out=uc, in_=ue[:, 4:4 + T], func=AF.Identity,
                                                             scale=C.lruv[:, l, 3, c:c + 1], bias=C.lruv[:, l, 5, c:c + 1]),
                   waits=[ld[c], prev_dve, C.lru_pe_free])
        for tap in range(3):
            off = 1 + tap
            t_c = b.op("dve", lambda e, ue=ue, tap=tap, off=off, c=c: e.scalar_tensor_tensor(
                out=uc, in0=ue[:, off:off + T], scalar=C.lruv[:, l, tap, c:c + 1], in1=uc, op0=ALU.mult, op1=ALU.add),
                waits=[t_c])
        u_free[s] = t_c
        last_act = None
        for (wt, dst, bidx, bank0) in ((wab, r, 6, 0), (wxb, ii, 7, 4)):
            for t8 in range(NT8):
                bank = bank0 + (t8 % 4)
                sl = slice(t8 * 512, (t8 + 1) * 512)
                t_m = b.op("pe", lambda e, bank=bank, wt=wt, sl=sl, c=c: e.matmul(C.ps[bank][:, :], lhsT=wt[:, c, :], rhs=uc[:, sl],
                                                                                   start=True, stop=True),
                           waits=[t_c, C.bank_free[bank], t_w1, t_w2])
                last_act = b.op("act", lambda e, bank=bank, dst=dst, sl=sl, bidx=bidx, c=c: e.activation(
                    out=dst[:, sl], in_=C.ps[bank][:, :], func=AF.Sigmoid, bias=C.lruv[:, l, bidx, c:c + 1]),
                    waits=[t_m, prev_dve])
                C.bank_free[bank] = last_act
                C.lru_pe_free = t_m
        t_a = b.op("act", lambda e, c=c: e.activation(out=a, in_=r, func=AF.Exp, scale=C.cp[:, c:c + 1]), waits=[t_cp, prev_dve])
        b.op("act", lambda e, c=c: e.activation(out=r, in_=r, func=AF.Exp, scale=C.cp2[:, c:c + 1]))
        t_s = b.op("act", lambda e: e.activation(out=r, in_=r, func=AF.Sqrt, scale=-1.0, bias=C.one_col[:, 0:1]))
        b.op("dve", lambda e: e.tensor_tensor(out=ii, in0=ii, in1=r, op=ALU.mult), waits=[t_s])
        b.op("dve", lambda e: e.tensor_tensor(out=ii, in0=ii, in1=uc, op=ALU.mult))
        hs = c % 2
        prev_dve = b.op("dve", lambda e, hs=hs: e.tensor_tensor_scan(out=hout[hs], data0=a, data1=ii, initial=0.0,
                                                                      op0=ALU.mult, op1=ALU.add), waits=[t_a, h_free[hs]])
        h_free[hs] = b.dma("sp", S["hl_s"][c * 128:(c + 1) * 128, 0:T], hout[hs], C.misc_sem[2 + hs], waits=[prev_dve])
    b.barrier()


def mix_attn(C, l, S):
    b = C.b
    A = C.arena
    T = S["T"]
    NQT = T // 128
    NB = T // MB
    o = 0
    KT = [A.at(o + s * T * 2, BF16, [T]) for s in range(2)]; o += 2 * T * 2
    QT = [A.at(o + s * T * 2, BF16, [T]) for s in range(2)]; o += 2 * T * 2
    V = [A.at(o + s * T * 2, BF16, [NQT, 128]) for s in range(2)]; o += 2 * T * 2
    ost = [A.at(o + s * T * 2, BF16, [T]) for s in range(2)]; o += 2 * T * 2
    biasT = A.at(o, BF16, [T]); o += T * 2
    gm = A.at(o, F32, [NQT, 16]); o += NQT * 64
    selt = A.at(o, F32, [NQT, 16]); o += NQT * 64
    top8 = A.at(o, F32, [NQT, 8]); o += NQT * 32
    biasq = A.at(o, BF16, [NQT, 16]); o += NQT * 32
    PT = [A.at(o + s * 1024, BF16, [512]) for s in range(4)]; o += 4096
    rsb = [A.at(o + s * 1024, F32, [256]) for s in range(2)]; o += 2048
    kmb = A.at(o, BF16, [NH, 16]); o += NH * 32
    masks = A.at(o, F32, [3, 32, 16]); o += 3 * 32 * 16 * 4
    C.mE1, C.mPB, C.mE2 = masks[:, 0], masks[:, 1], masks[:, 2]
    C.sel_b = A.at(o, BF16, [2048]); o += 4096
    C.causal_b = A.at(o, BF16, [2, 256]); o += 1024
    assert o <= A.nbytes
    scale = float(HD) ** -0.5
    t_c1 = b.dma("sp", masks, S["masks_h"], C.misc_sem[2])
    t_c2 = b.dma("pool", C.sel_b[0:16, :], S["sel_h"], C.misc_sem[3])
    t_c3 = b.dma("pool", C.causal_b, S["causal_h"], C.misc_sem[4])
    b.wait_only("pe", [t_c2, t_c3])
    b.wait_only("dve", [t_c1])
    t_km = b.op("dve", lambda e: e.tensor_scalar_mul(out=kmb, in0=C.kmean_f[:, :, :], scalar1=1.0 / MB))
    ld_sem = C.w13_sem
    h_free = [None, None]
    o_free = [None, None]

    def load(h):
        s = h % 2
        w = [h_free[s]]
        b.dma("sp", KT[s], S["k_s"][h * 128:(h + 1) * 128, 0:T], ld_sem[s], waits=w)
        b.dma("sp", QT[s], S["q_s"][h * 128:(h + 1) * 128, 0:T], ld_sem[s], waits=w)
        return b.dma("sp", V[s], S["v_s"].rearrange("(i p) c -> p i c", p=128)[:, 0:NQT, h * 128:(h + 1) * 128], ld_sem[s], waits=w)

    ld = {0: load(0)}
    pt_free = [None] * 4
    pti = 0
    rs_free = [None, None]
    accp = 0
    bias_free = None
    for h in range(NH):
        s = h % 2
        if h + 1 < NH:
            ld[h + 1] = load(h + 1)
        kt_, qt_, v_ = KT[s], QT[s], V[s]
        t_g = None
        for qt in range(NQT):
            w = [ld[h], t_km, C.bank_free[7]] if qt == 0 else []
            t_g = b.op("pe", lambda e, qt=qt, qt_=qt_, h=h: e.matmul(C.ps[7][:, qt * 16:(qt + 1) * 16], lhsT=qt_[:, qt * 128:(qt + 1) * 128],
                                                                     rhs=kmb[:, h, :], start=True, stop=True), waits=w,
                       signal=(qt == NQT - 1))
        psG = C.ps[7][:, 0:NQT * 16].rearrange("p (a b) -> p a b", b=16)
        t_1 = b.op("dve", lambda e, psG=psG: e.tensor_tensor(out=gm, in0=psG, in1=C.mE1[:, 0:NQT, :], op=ALU.add), waits=[t_g])
        C.bank_free[7] = t_1
        for qt in range(NQT):
            b.op("dve", lambda e, qt=qt: e.max(out=top8[:, qt, :], in_=gm[:, qt, :]), signal=False)
        b.op("dve", lambda e: e.tensor_tensor(out=selt, in0=gm, in1=top8[:, :, 2:3].to_broadcast([128, NQT, 16]), op=ALU.is_ge))
        b.op("dve", lambda e: e.scalar_tensor_tensor(out=selt, in0=selt, scalar=-1.0, in1=C.mPB[:, 0:NQT, :], op0=ALU.add, op1=ALU.mult))
        t_bq = b.op("dve", lambda e: e.tensor_tensor(out=biasq, in0=selt, in1=C.mE2[:, 0:NQT, :], op=ALU.add), waits=[bias_free])
        t_t = None
        for qt in range(NQT):
            bank = (qt * 128) // 1024
            col = (qt * 128) % 1024
            pv = C.ps[bank][:, :].bitcast(BF16)
            w = [t_bq, C.bank_free[bank]] if col == 0 else []
            t_t = b.op("pe", lambda e, pv=pv, col=col, qt=qt: e.transpose(out=pv[0:16, col:col + 128], in_=biasq[:, qt, :],
                                                                        identity=C.ident_b[:, :]), waits=w,
                       signal=(col == 896 or qt == NQT - 1))
            if col == 896 or qt == NQT - 1:
                t_cp = b.op("act", lambda e, pv=pv, bank=bank: e.activation(
                    out=biasT[0:16, bank * 1024:bank * 1024 + min(1024, T - bank * 1024)],
                    in_=pv[0:16, 0:min(1024, T - bank * 1024)], func=AF.Copy), waits=[t_t, bias_free])
                C.bank_free[bank] = t_cp
        t_bias = t_cp
        os_ = ost[s]
        last_pe = None
        for j in range(NB):
            ob = 4 + 2 * (accp % 2)
            rb = ob + 1
            accp += 1
            qsl = slice(j * MB, (j + 1) * MB)
            for n in range(j + 1):
                sb_ = (pti % 4)
                bk = sb_
                pti += 1
                for k2 in range(2):
                    kt = 2 * n + k2
                    csl = slice(k2 * 256, (k2 + 1) * 256)
                    w = [C.bank_free[bk], t_bias] if k2 == 0 else []
                    b.op("pe", lambda e, bk=bk, csl=csl, kt=kt, qsl=qsl, kt_=kt_, qt_=qt_: e.matmul(
                        C.ps[bk][:, csl], lhsT=kt_[:, kt * 128:(kt + 1) * 128], rhs=qt_[:, qsl], start=True, stop=False),
                        waits=w, signal=False)
                    diag = (n == j)
                    t_s = b.op("pe", lambda e, bk=bk, csl=csl, n=n, qsl=qsl, diag=diag: e.matmul(
                        C.ps[bk][:, csl], lhsT=C.sel_b[0:16, n * 128:(n + 1) * 128], rhs=biasT[0:16, qsl], start=False, stop=(not diag)),
                        signal=(not diag and k2 == 1))
                    if diag:
                        t_s = b.op("pe", lambda e, bk=bk, csl=csl, k2=k2: e.matmul(
                            C.ps[bk][:, csl], lhsT=C.ident_b[:, :], rhs=C.causal_b[:, k2, :], start=False, stop=True),
                            signal=(k2 == 1))
                pt = PT[sb_]
                t_e = b.op("act", lambda e, bk=bk, pt=pt: e.activation(out=pt, in_=C.ps[bk][:, :], func=AF.Exp, scale=scale),
                           waits=[t_s, pt_free[sb_]])
                C.bank_free[bk] = t_e
                for k2 in range(2):
                    kt = 2 * n + k2
                    csl = slice(k2 * 256, (k2 + 1) * 256)
                    first = (n == 0 and k2 == 0)
                    lastf = (n == j and k2 == 1)
                    w = [t_e] + ([C.bank_free[ob], C.bank_free[rb]] if first else [])
                    b.op("pe", lambda e, ob=ob, kt=kt, csl=csl, pt=pt, v_=v_, first=first, lastf=lastf: e.matmul(
                        C.ps[ob][:, 0:256], lhsT=v_[:, kt, :], rhs=pt[:, csl], start=first, stop=lastf), waits=w, signal=False)
                    last_pe = b.op("pe", lambda e, rb=rb, csl=csl, pt=pt, first=first, lastf=lastf: e.matmul(
                        C.ps[rb][:, 0:256], lhsT=C.ones_b[:, :], rhs=pt[:, csl], start=first, stop=lastf), signal=(k2 == 1))
                pt_free[sb_] = last_pe
            ri = j % 2
            t_r = b.op("act", lambda e, rb=rb, ri=ri: e.activation(out=rsb[ri], in_=C.ps[rb][:, 0:256], func=AF.Copy),
                       waits=[last_pe, rs_free[ri]])
            C.bank_free[rb] = t_r
            b.op("dve", lambda e, ri=ri: e.reciprocal(out=rsb[ri], in_=rsb[ri]), waits=[t_r])
            t_o = b.op("dve", lambda e, ob=ob, ri=ri, os_=os_, qsl=qsl: e.tensor_tensor(out=os_[:, qsl], in0=C.ps[ob][:, 0:256], in1=rsb[ri],
                                                                                      op=ALU.mult), waits=[last_pe, o_free[s]])
            C.bank_free[ob] = t_o
            rs_free[ri] = t_o
        bias_free = last_pe
        h_free[s] = last_pe
        o_free[s] = b.dma("sp", S["o_s"][h * 128:(h + 1) * 128, 0:T], os_, C.misc_sem[s], waits=[t_o])
    b.barrier()


def mix_out_block(C, xin, xout, t0, l, w_in, w_lu, w_au, w_out, S):
    b = C.b
    A = C.arena
    xin_v = xin.rearrange("(kc p) t -> p kc t", p=128)
    xout_v = xout.rearrange("(kc p) t -> p kc t", p=128)
    wiv = w_in.rearrange("(kc p) n -> p kc n", p=128)
    wluv = w_lu.rearrange("(kc p) n -> p kc n", p=128)
    wauv = w_au.rearrange("(kc p) n -> p kc n", p=128)
    wov = w_out.rearrange("(kc p) n -> p kc n", p=128)
    o = 0
    hn = A.at(0, BF16, [KC, TB])
    hl = A.at(32768, BF16, [8, TB])
    oa = A.at(49152, BF16, [8, TB])
    y = A.at(0, F32, [KC, TB])
    o = 65536
    mg = A.at(o, BF16, [KC, TB]); o += 32768
    wsl = []
    for s in range(2):
        wsl.append((A.at(o, BF16, [KC, 128]), A.at(o + 4096, BF16, [KC, 128]), A.at(o + 8192, BF16, [8, 128]),
                    A.at(o + 10240, BF16, [8, 128])))
        o += 12288
    wos = [A.at(o + s * 4096, BF16, [KC, 128]) for s in range(3)]; o += 3 * 4096
    xs = [A.at(o + s * 4096, F32, [TB]) for s in range(3)]; o += 3 * 4096
    rstd = A.at(o, F32, [TB]); o += 4096
    sq = [A.at(o + s * 4096, F32, [TB]) for s in range(2)]; o += 8192
    tmps = [A.at(o + s * 2048, F32, [512]) for s in range(6)]; o += 6 * 2048
    assert o <= A.nbytes

    t_hl = b.dma("sp", hl, S["hl_s"].rearrange("(c p) t -> p c t", p=128)[:, :, t0:t0 + TB], C.misc_sem[0])
    t_oa = b.dma("sp", oa, S["o_s"].rearrange("(c p) t -> p c t", p=128)[:, :, t0:t0 + TB], C.misc_sem[1])
    t_xn = emit_prenorm(C, xin_v, t0, l * 6 + 2, hn, xs, rstd, sq, psS=[6, 7])

    wo_loaded = {}

    def wo_load(m, s, waits):
        return [b.dma("pool", wos[s], wov[:, :, m * 128:(m + 1) * 128], C.w2_sem[s], waits=waits)]

    tmp_free = [None] * 6
    st = {"i": 0}

    def load(m, s, waits):
        b.dma("pool", wsl[s][0], wiv[:, :, 4096 + m * 128:4096 + (m + 1) * 128], C.w13_sem[s], waits=waits)
        b.dma("pool", wsl[s][1], wiv[:, :, 6144 + m * 128:6144 + (m + 1) * 128], C.w13_sem[s], waits=waits)
        b.dma("pool", wsl[s][2], wluv[:, :, m * 128:(m + 1) * 128], C.w13_sem[s], waits=waits)
        return [b.dma("pool", wsl[s][3], wauv[:, :, m * 128:(m + 1) * 128], C.w13_sem[s], waits=waits)]

    def compute(m, s, ld):
        last = None
        first = [True]
        for tt in range(NTT):
            base = (st["i"] % 2) * 4
            st["i"] += 1
            sl = slice(tt * 512, (tt + 1) * 512)
            toks = []
            for gi, (src, nk) in enumerate(((hn, KC), (hn, KC), (hl, 8), (oa, 8))):
                bank = base + gi
                wt = wsl[s][gi]
                for kc in range(nk):
                    w = []
                    if kc == 0:
                        w = [C.bank_free[bank]]
                        if first[0]:
                            w += list(ld) + ([t_xn[-1], t_hl, t_oa] if m == 0 else [])
                            first[0] = False
                    last = b.op("pe", lambda e, bank=bank, wt=wt, kc=kc, src=src, sl=sl: e.matmul(
                        C.ps[bank][:, :], lhsT=wt[:, kc, :], rhs=src[:, kc, sl], start=(kc == 0), stop=(kc == nk - 1)),
                        waits=w, signal=(kc == nk - 1))
                toks.append(last)
            ti = (st["i"] % 2) * 3
            sa, sb_, t1 = tmps[ti], tmps[ti + 1], tmps[ti + 2]
            t_sa = b.op("act", lambda e, base=base, sa=sa, m=m: e.activation(out=sa, in_=C.ps[base][:, :], func=AF.Sigmoid,
                                                                           bias=C.bg[:, l, m:m + 1]), waits=[toks[0], tmp_free[ti]])
            C.bank_free[base] = t_sa
            t_sb = b.op("act", lambda e, base=base, sb_=sb_, m=m: e.activation(out=sb_, in_=C.ps[base + 1][:, :], func=AF.Sigmoid,
                                                                             bias=C.bg[:, l, 16 + m:17 + m]), waits=[toks[1], tmp_free[ti + 1]])
            C.bank_free[base + 1] = t_sb
            t_1 = b.op("dve", lambda e, base=base, sa=sa, t1=t1: e.tensor_tensor(out=t1, in0=sa, in1=C.ps[base + 2][:, :], op=ALU.mult),
                       waits=[t_sa, toks[2], tmp_free[ti + 2]])
            C.bank_free[base + 2] = t_1
            tmp_free[ti] = t_1
            t_2 = b.op("dve", lambda e, base=base, sb_=sb_: e.tensor_tensor(out=sb_, in0=sb_, in1=C.ps[base + 3][:, :], op=ALU.mult),
                       waits=[t_sb, toks[3]])
            C.bank_free[base + 3] = t_2
            t_3 = b.op("dve", lambda e, t1=t1, sb_=sb_, m=m, sl=sl: e.tensor_tensor(out=mg[:, m, sl], in0=t1, in1=sb_, op=ALU.add))
            tmp_free[ti + 1] = t_3
            tmp_free[ti + 2] = t_3
            st["mg"] = t_3
        if m == 0:
            for mm in range(3):
                wo_loaded[mm] = wo_load(mm, mm % 3, [])
        return last

    stream(KC, 2, load, compute)

    psS = [6, 7]
    stat = {"last": None}
    for m in range(KC):
        s = m % 3
        ld = wo_loaded.pop(m)
        par = m % 2
        psY = [par * 2 + tt for tt in range(NTT)]
        last = None
        for kc in range(KC):
            for tt in range(NTT):
                w = []
                if kc == 0:
                    w = list(ld) + [C.bank_free[psY[tt]]] + ([st["mg"]] if m == 0 else [])
                last = b.op("pe", lambda e, tt=tt, kc=kc, s=s, bank=psY[tt]: e.matmul(
                    C.ps[bank][:, :], lhsT=wos[s][:, kc, :], rhs=mg[:, kc, tt * 512:(tt + 1) * 512],
                    start=(kc == 0), stop=(kc == KC - 1)), waits=w, signal=(kc == KC - 1))
        stat["last"] = emit_outproj_stats(C, m, psY, psS, y, sq, last)
        if m + 3 < KC:
            wo_loaded[m + 3] = wo_load(m + 3, s, [last])
    t_r = emit_stats_finish(C, psS, rstd, [stat["last"]])
    emit_post_residual(C, y, l * 6 + 3, rstd, t_r, xin_v, xout_v, t0, xs, 1.0)
    b.barrier()


def build_program(T=SEQ, phases=None, NL=DEPTH):
    if phases is None:
        phases = []
        for l in range(NL):
            phases += ["ffn%d_0" % l, "mix%d" % l, "ffn%d_1" % l]
    nc = bass.Bass("TRN2", target_bir_lowering=False)
    C = Ctx()
    C.nc = nc
    dt = nc.dram_tensor

    def ext(name, shape):
        return dt(name, list(shape), F32, kind="ExternalInput").ap()

    xT = ext("xT", [D, T])
    ng_h = ext("ng_h", [128, NL * 6, KC])
    w13 = ext("ffn_w13", [NL, 2, D, 2 * DFF])
    w2 = ext("ffn_w2", [NL, 2, DFF, D])
    w_in = ext("w_in", [NL, D, IN_COLS])
    w_lu = ext("w_lru_up", [NL, D_LRU, D])
    w_au = ext("w_attn_up", [NL, D_ATTN, D])
    w_out = ext("w_out", [NL, D, D])
    bg_h = ext("bg_h", [128, NL, 32])
    lruv_h = ext("lruv_h", [128, NL, 8, 8])
    wa_bd = ext("wa_bd", [NL, 128, 8, 128])
    wx_bd = ext("wx_bd", [NL, 128, 8, 128])
    ropeC = ext("ropeC", [32, T])
    ropeS = ext("ropeS", [32, T])
    perm_h = ext("perm_h", [32, 32])
    masks_h = ext("masks_h", [128, 3, 32, 16])
    sel_h = ext("sel_h", [16, 2048])
    causal_h = ext("causal_h", [128, 2, 256])
    ident_h = ext("ident_h", [128, 128])
    outT = dt("outT", [D, T], F32, kind="ExternalOutput").ap()
    xres = dt("xres", [D, T], F32).ap()
    S = {"masks_h": masks_h, "sel_h": sel_h, "causal_h": causal_h, "T": T, "ropeC": ropeC, "ropeS": ropeS, "wa_bd": wa_bd, "wx_bd": wx_bd,
         "u_s": dt("u_s", [D_LRU, T], F32).ap(), "q_s": dt("q_s", [D_ATTN, T], BF16).ap(),
         "k_s": dt("k_s", [D_ATTN, T], BF16).ap(), "v_s": dt("v_s", [T, D_ATTN], BF16).ap(),
         "hl_s": dt("hl_s", [D_LRU, T], BF16).ap(), "o_s": dt("o_s", [D_ATTN, T], BF16).ap()}

    b = Builder(nc)
    C.b = b
    global LAST_B
    LAST_B = b
    es = contextlib.ExitStack()
    with es:
        ARENA_BYTES = 202752
        sb = lambda name, shape, dtp: es.enter_context(nc.sbuf_tensor(name, shape, dtp))
        ah = sb("arena", [128, ARENA_BYTES // 2], BF16)
        C.arena = Arena(ah, ARENA_BYTES)
        C.ng = sb("ng", [128, NL * 6, KC], F32)
        C.bg = sb("bg", [128, NL, 32], F32)
        C.lruv = sb("lruv", [128, NL, 8, 8], F32)
        C.cp = sb("cp", [128, 8], F32)
        C.cp2 = sb("cp2", [128, 8], F32)
        C.kmean_f = sb("kmean_f", [128, NH, 16], F32)
        C.ones_f = sb("ones_f", [128, 128], F32)
        C.ones_b = sb("ones_b", [128, 128], BF16)
        C.ident_b = sb("ident_b", [128, 128], BF16)
        C.perm = sb("perm", [32, 32], F32)
        C.eps = sb("eps", [128, 1], F32)
        C.one_col = sb("one_col", [128, 1], F32)
        C.ps = [es.enter_context(nc.psum_tensor("ps%d" % i, [128, 512], F32)) for i in range(8)]
        C.bank_free = [None] * 8
        C.xs_sem = [b.new_dsem() for _ in range(3)]
        C.xs_free = [None] * 3
        C.sq_free = [None] * 2
        C.w13_sem = [b.new_dsem() for _ in range(3)]
        C.w2_sem = [b.new_dsem() for _ in range(3)]
        C.misc_sem = [b.new_dsem() for _ in range(8)]
        C.lru_pe_free = None
        csem = b.new_dsem()

        toks = [b.dma("sp", C.ng[:, :, :], ng_h, csem), b.dma("sp", C.bg[:, :, :], bg_h, csem),
                b.dma("sp", C.lruv[:, :, :, :], lruv_h, csem), b.dma("sp", C.perm[:, :], perm_h, csem),
                b.dma("pool", C.ident_b[:, :], ident_h, csem)]
        b.op("dve", lambda e: e.memset(C.ones_f[:, :], 1.0))
        b.op("dve", lambda e: e.memset(C.ones_b[:, :], 1.0))
        b.op("dve", lambda e: e.memset(C.one_col[:, :], 1.0))
        b.op("dve", lambda e: e.memset(C.kmean_f[:, :, :], 0.0))
        b.op("dve", lambda e: e.memset(C.eps[:, :], EPS), waits=toks)
        b.barrier()

        cur = xT
        plist = list(phases)
        for pi, ph in enumerate(plist):
            dst = outT if pi == len(plist) - 1 else xres
            head = ph.split("_")[0]
            kind, l = head[:3], int(head[3:])
            if kind == "ffn":
                i = int(ph.split("_")[1])
                for tb in range(T // TB):
                    ffn_block(C, cur, dst, tb * TB, l * 6 + (0 if i == 0 else 4), l * 6 + (1 if i == 0 else 5),
                              w13[l, i], w2[l, i])
            elif kind == "mix":
                for tb in range(T // TB):
                    mix_inproj_block(C, cur, tb * TB, l, w_in[l], S)
                mix_lru(C, l, S)
                mix_attn(C, l, S)
                for tb in range(T // TB):
                    mix_out_block(C, cur, dst, tb * TB, l, w_in[l], w_lu[l], w_au[l], w_out[l], S)
            cur = dst
        b.barrier()
        b.emit()
    return nc


def host_consts(T):
    pos = np.arange(T, dtype=np.float32)
    inv = (np.float32(500000.0) ** (-np.arange(0, 32, 2, dtype=np.float32) / np.float32(32))).astype(np.float32)
    ang = (pos[:, None] * inv[None, :]).astype(np.float32)
    cos = np.cos(ang).astype(np.float32).T
    sin = np.sin(ang).astype(np.float32).T
    ropeC = np.ascontiguousarray(np.concatenate([cos, cos], axis=0))
    ropeS = np.ascontiguousarray(np.concatenate([-sin, sin], axis=0))
    perm = np.zeros((32, 32), np.float32)
    for dst in range(32):
        perm[(dst + 16) % 32, dst] = 1.0
    masks = np.zeros((128, 3, 32, 16), np.float32)
    for qt in range(32):
        j = qt // 2
        for n in range(16):
            masks[:, 0, qt, n] = 0.0 if n < j else -1e30
            masks[:, 1, qt, n] = -NEG if n < j else 0.0
            masks[:, 2, qt, n] = 0.0 if n <= j else NEG
    sel = np.zeros((16, 16, 128), np.float32)
    for n in range(16):
        sel[n, n, :] = 1.0
    causal = np.zeros((128, 2, 256), np.float32)
    for k2 in range(2):
        kk = k2 * 128 + np.arange(128)[:, None]
        qq = np.arange(256)[None, :]
        causal[:, k2, :] = np.where(kk > qq, NEG, 0.0)
    return {"ropeC": ropeC, "ropeS": ropeS, "perm_h": perm, "masks_h": masks, "sel_h": sel.reshape(16, 2048),
            "causal_h": causal, "ident_h": np.eye(128, dtype=np.float32)}


def host_layout(inputs, T=SEQ, ncores=BATCH, l0=0, NL=DEPTH, x_T=None):
    f = lambda k: np.asarray(inputs[k], dtype=np.float32)[l0:l0 + NL]
    ng_h = np.ascontiguousarray(f("norm_gains").reshape(NL * 6, KC, 128).transpose(2, 0, 1))
    bg_h = np.ascontiguousarray(f("b_gate").reshape(NL, 32, 128).transpose(2, 0, 1))
    lruv = np.zeros((128, NL, 8, 8), np.float32)
    lruv[:, :, 0:4, :] = f("conv_w").reshape(NL, 4, 8, 128).transpose(3, 0, 1, 2)
    lruv[:, :, 4, :] = f("lru_lambda").reshape(NL, 8, 128).transpose(2, 0, 1)
    lruv[:, :, 5, :] = f("conv_b").reshape(NL, 8, 128).transpose(2, 0, 1)
    lruv[:, :, 6, :] = f("lru_ba").reshape(NL, 8, 128).transpose(2, 0, 1)
    lruv[:, :, 7, :] = f("lru_bx").reshape(NL, 8, 128).transpose(2, 0, 1)

    def blockdiag(w):
        o = np.zeros((NL, 128, 8, 128), np.float32)
        for c in range(8):
            o[:, 0:64, c, 0:64] = w[:, 2 * c]
            o[:, 64:128, c, 64:128] = w[:, 2 * c + 1]
        return o

    common = {"ng_h": ng_h, "bg_h": bg_h, "lruv_h": lruv, "wa_bd": blockdiag(f("lru_wa")), "wx_bd": blockdiag(f("lru_wx")),
              "ffn_w13": f("ffn_w13"), "ffn_w2": f("ffn_w2"), "w_in": f("w_in"), "w_lru_up": f("w_lru_up"),
              "w_attn_up": f("w_attn_up"), "w_out": f("w_out")}
    common.update(host_consts(T))
    maps = []
    for c in range(ncores):
        m = dict(common)
        if x_T is not None:
            m["xT"] = x_T[c]
        else:
            m["xT"] = np.ascontiguousarray(np.asarray(inputs["x"], dtype=np.float32)[c, :T].T)
        maps.append(m)
    return maps


N_LAUNCH_LAYERS = 2


def kernel(**inputs):
    NL = N_LAUNCH_LAYERS
    nc = build_program(SEQ, None, NL)
    x_T = None
    for l0 in range(0, DEPTH, NL):
        maps = host_layout(inputs, SEQ, BATCH, l0, NL, x_T)
        res = run_bass_kernel_spmd(nc, maps, core_ids=list(range(BATCH)))
        x_T = [res.results[c]["outT"] for c in range(BATCH)]
    out = np.stack([np.ascontiguousarray(x_T[c].T) for c in range(BATCH)], axis=0)
    return out.astype(np.float32)
```
